# Optimizing a Trainium2 kernel written in Bass

```python
import jax, jax.numpy as jnp
from jax import lax
import numpy as np

D_MODEL = 1024
BATCH = 8
SEQ = 2048
DEPTH = 2

N_HEADS = 16
N_KV_GROUPS = 2
HEADS_PER_GROUP = N_HEADS // N_KV_GROUPS
HEAD_DIM = 64
N_KV_PARTS = 6
N_BRANCH = 3
L_CMP = 32
STRIDE_CMP = 16
L_SLC = 64
N_SEL = 16
WINDOW = 512
Q_CHUNK = 64
CONV_WIDTH = 31
FFN_HIDDEN = -(-(8 * D_MODEL) // (3 * 256)) * 256
EPS = 1e-6
NEG = -1e30
FORCE = 1e9

kernel_name = "yoco_conformer_conv_nsa_hybrid"


def rmsnorm(x, g):
    xf = x.astype(jnp.float32)
    y = xf * lax.rsqrt(jnp.mean(xf * xf, axis=-1, keepdims=True) + EPS)
    return (y * g.astype(jnp.float32)).astype(x.dtype)


def layernorm(x, g, b):
    xf = x.astype(jnp.float32)
    mu = jnp.mean(xf, axis=-1, keepdims=True)
    var = jnp.mean(jnp.square(xf - mu), axis=-1, keepdims=True)
    y = (xf - mu) * lax.rsqrt(var + EPS)
    return (y * g.astype(jnp.float32) + b.astype(jnp.float32)).astype(x.dtype)


def swiglu_ffn(x, norm_g, w_in, w_out):
    gate, up = jnp.split(rmsnorm(x, norm_g) @ w_in, 2, axis=-1)
    return (jax.nn.silu(gate) * up) @ w_out


def conformer_conv(x, norm_g, pw1_w, pw1_b, dw_w, dw_b, ln_g, ln_b, pw2_w, pw2_b):
    a, b = jnp.split(rmsnorm(x, norm_g) @ pw1_w + pw1_b, 2, axis=-1)
    h = a * jax.nn.sigmoid(b)
    h = lax.conv_general_dilated(
        h, dw_w[:, None, :], window_strides=(1,),
        padding=[(CONV_WIDTH - 1, 0)],
        dimension_numbers=('NWC', 'WIO', 'NWC'),
        feature_group_count=h.shape[-1]) + dw_b
    h = jax.nn.silu(layernorm(h, ln_g, ln_b))
    return h @ pw2_w + pw2_b


def shared_kv(h, kv_norm, w_kv, cmp_pos_k, cmp_pos_v, phi_k_w1, phi_k_w2, phi_v_w1, phi_v_w2):
    B, S, _ = h.shape
    kv = (rmsnorm(h, kv_norm) @ w_kv).reshape(B, S, N_KV_PARTS, N_KV_GROUPS, HEAD_DIM)
    kv = kv.transpose(2, 0, 3, 1, 4)
    k_cmp_raw, v_cmp_raw, k_slc, v_slc, k_win, v_win = kv
    n_cmp = (S - L_CMP) // STRIDE_CMP + 1
    idx = np.arange(n_cmp)[:, None] * STRIDE_CMP + np.arange(L_CMP)[None, :]

    def compress(raw, pos, w1, w2):
        blk = raw[:, :, idx] + pos
        blk = blk.reshape(B, N_KV_GROUPS, n_cmp, L_CMP * HEAD_DIM)
        return jax.nn.silu(blk @ w1) @ w2

    k_cmp = compress(k_cmp_raw, cmp_pos_k, phi_k_w1, phi_k_w2)
    v_cmp = compress(v_cmp_raw, cmp_pos_v, phi_v_w1, phi_v_w2)
    n_slc = S // L_SLC
    k_slc_blk = k_slc.reshape(B, N_KV_GROUPS, n_slc, L_SLC, HEAD_DIM)
    v_slc_blk = v_slc.reshape(B, N_KV_GROUPS, n_slc, L_SLC, HEAD_DIM)
    pad = ((0, 0), (0, 0), (WINDOW, 0), (0, 0))
    k_win_pad = jnp.pad(k_win, pad)
    v_win_pad = jnp.pad(v_win, pad)
    return (k_cmp, v_cmp, k_slc_blk, v_slc_blk, k_win_pad, v_win_pad)


def nsa_attention(x, norm_g, w_in, w_out, kv):
    k_cmp, v_cmp, k_slc_blk, v_slc_blk, k_win_pad, v_win_pad = kv
    B, S, _ = x.shape
    G, Hg, dh = N_KV_GROUPS, HEADS_PER_GROUP, HEAD_DIM
    n_chunk = S // Q_CHUNK
    n_cmp = k_cmp.shape[2]
    n_slc = k_slc_blk.shape[2]
    k_sel = min(N_SEL, n_slc)
    scale = HEAD_DIM ** -0.5

    proj = rmsnorm(x, norm_g) @ w_in
    q = proj[..., :N_HEADS * dh]
    gates = jax.nn.sigmoid(proj[..., N_HEADS * dh:])
    q = q.reshape(B, n_chunk, Q_CHUNK, G, Hg, dh).transpose(1, 0, 3, 4, 2, 5)
    gates = gates.reshape(B, n_chunk, Q_CHUNK, N_BRANCH, G, Hg).transpose(1, 3, 0, 4, 5, 2)
    starts = jnp.arange(n_chunk, dtype=jnp.int32) * Q_CHUNK

    slopes = jnp.asarray(2.0 ** (-8.0 * (np.arange(N_HEADS) + 1) / N_HEADS), jnp.float32)
    slopes = slopes.reshape(G, Hg)[:, :, None, None]
    cmp_end = jnp.arange(n_cmp, dtype=jnp.int32) * STRIDE_CMP + (L_CMP - 1)
    cs = np.arange(n_cmp)[:, None] * STRIDE_CMP
    ss = np.arange(n_slc)[None, :] * L_SLC
    overlap = jnp.asarray((cs <= ss + L_SLC - 1) & (cs + L_CMP - 1 >= ss), jnp.float32)
    blk = jnp.arange(n_slc, dtype=jnp.int32)
    bidx = jnp.arange(B)[:, None, None, None]
    gidx = jnp.arange(G)[None, :, None, None]

    def chunk_fn(args):
        q_c, g_c, c0 = args
        t = c0 + jnp.arange(Q_CHUNK, dtype=jnp.int32)

        dist = (t[:, None] - cmp_end[None, :]).astype(jnp.float32)
        valid = dist >= 0
        s = jnp.einsum('bghqd,bgnd->bghqn', q_c, k_cmp).astype(jnp.float32) * scale
        s = jnp.where(valid, s - slopes * dist, NEG)
        p_cmp = jax.nn.softmax(s, axis=-1) * valid.any(-1, keepdims=True)
        o_cmp = jnp.einsum('bghqn,bgnd->bghqd', p_cmp.astype(v_cmp.dtype), v_cmp)

        imp = jnp.einsum('bghqn,nj->bgqj', p_cmp, overlap)
        t_blk = t // L_SLC
        forced = (blk[None] == 0) | (blk[None] == t_blk[:, None]) | (blk[None] == t_blk[:, None] - 1)
        causal = blk[None] * L_SLC <= t[:, None]
        imp = jnp.where(forced, FORCE, jnp.where(causal, imp, -FORCE))
        _, sel = lax.top_k(imp, k_sel)
        ks = k_slc_blk[bidx, gidx, sel]
        vs = v_slc_blk[bidx, gidx, sel]
        pos = sel[..., None] * L_SLC + jnp.arange(L_SLC, dtype=jnp.int32)
        dist = (t[None, None, :, None, None] - pos).astype(jnp.float32)[:, :, None]
        s = jnp.einsum('bghqd,bgqkld->bghqkl', q_c, ks).astype(jnp.float32) * scale
        s = jnp.where(dist >= 0, s - slopes[..., None] * dist, NEG)
        p = jax.nn.softmax(s.reshape(B, G, Hg, Q_CHUNK, k_sel * L_SLC), axis=-1)
        o_slc = jnp.einsum('bghqm,bgqmd->bghqd', p.astype(vs.dtype),
                           vs.reshape(B, G, Q_CHUNK, k_sel * L_SLC, dh))

        kw = lax.dynamic_slice_in_dim(k_win_pad, c0, WINDOW + Q_CHUNK, axis=2)
        vw = lax.dynamic_slice_in_dim(v_win_pad, c0, WINDOW + Q_CHUNK, axis=2)
        s_pos = c0 - WINDOW + jnp.arange(WINDOW + Q_CHUNK, dtype=jnp.int32)
        dist = (t[:, None] - s_pos[None, :]).astype(jnp.float32)
        ok = (dist >= 0) & (dist < WINDOW) & (s_pos[None, :] >= 0)
        s = jnp.einsum('bghqd,bgmd->bghqm', q_c, kw).astype(jnp.float32) * scale
        s = jnp.where(ok, s - slopes * dist, NEG)
        p = jax.nn.softmax(s, axis=-1)
        o_win = jnp.einsum('bghqm,bgmd->bghqd', p.astype(vw.dtype), vw)

        o = (g_c[0][..., None] * o_cmp + g_c[1][..., None] * o_slc
             + g_c[2][..., None] * o_win)
        return o.transpose(0, 3, 1, 2, 4).reshape(B, Q_CHUNK, N_HEADS * dh).astype(x.dtype)

    out = lax.map(chunk_fn, (q, gates, starts))
    out = out.transpose(1, 0, 2, 3).reshape(B, S, N_HEADS * dh)
    return out @ w_out


def setup_inputs(seed: int = 0) -> dict:
    key = jax.random.key(seed)
    ks = jax.random.split(key, 32)
    n_a = DEPTH // 2
    n_b = DEPTH - n_a
    D, F = D_MODEL, FFN_HIDDEN
    f32 = jnp.float32

    def w(k, shape, fan_in):
        return jax.random.normal(k, shape, f32) * fan_in ** -0.5

    def gain(k, shape):
        return 1.0 + 0.05 * jax.random.normal(k, shape, f32)

    def bias(k, shape):
        return 0.02 * jax.random.normal(k, shape, f32)

    return {
        "x": jax.random.normal(ks[0], (BATCH, SEQ, D), f32),
        "a_norm": gain(ks[1], (n_a, D)),
        "a_pw1_w": w(ks[2], (n_a, D, 2 * D), D),
        "a_pw1_b": bias(ks[3], (n_a, 2 * D)),
        "a_dw_w": w(ks[4], (n_a, CONV_WIDTH, D), CONV_WIDTH),
        "a_dw_b": bias(ks[5], (n_a, D)),
        "a_ln_g": gain(ks[6], (n_a, D)),
        "a_ln_b": bias(ks[7], (n_a, D)),
        "a_pw2_w": w(ks[8], (n_a, D, D), D),
        "a_pw2_b": bias(ks[9], (n_a, D)),
        "kv_norm": gain(ks[10], (D,)),
        "w_kv": w(ks[11], (D, N_KV_PARTS * N_KV_GROUPS * HEAD_DIM), D),
        "cmp_pos_k": 0.1 * jax.random.normal(ks[12], (L_CMP, HEAD_DIM), f32),
        "cmp_pos_v": 0.1 * jax.random.normal(ks[13], (L_CMP, HEAD_DIM), f32),
        "phi_k_w1": w(ks[14], (L_CMP * HEAD_DIM, HEAD_DIM), L_CMP * HEAD_DIM),
        "phi_k_w2": w(ks[15], (HEAD_DIM, HEAD_DIM), HEAD_DIM),
        "phi_v_w1": w(ks[16], (L_CMP * HEAD_DIM, HEAD_DIM), L_CMP * HEAD_DIM),
        "phi_v_w2": w(ks[17], (HEAD_DIM, HEAD_DIM), HEAD_DIM),
        "b_norm": gain(ks[18], (n_b, D)),
        "b_w_in": w(ks[19], (n_b, D, N_HEADS * HEAD_DIM + N_BRANCH * N_HEADS), D),
        "b_w_out": w(ks[20], (n_b, N_HEADS * HEAD_DIM, D), N_HEADS * HEAD_DIM),
        "ffn_norm": gain(ks[21], (DEPTH, D)),
        "ffn_w_in": w(ks[22], (DEPTH, D, 2 * F), D),
        "ffn_w_out": w(ks[23], (DEPTH, F, D), F),
        "final_norm": gain(ks[24], (D,)),
    }


def reference(x, a_norm, a_pw1_w, a_pw1_b, a_dw_w, a_dw_b, a_ln_g, a_ln_b, a_pw2_w, a_pw2_b,
              kv_norm, w_kv, cmp_pos_k, cmp_pos_v, phi_k_w1, phi_k_w2, phi_v_w1, phi_v_w2,
              b_norm, b_w_in, b_w_out, ffn_norm, ffn_w_in, ffn_w_out, final_norm):
    n_a = DEPTH // 2
    h = x
    kv = None
    for layer in range(DEPTH):
        if layer < n_a:
            i = layer
            h = h + conformer_conv(h, a_norm[i], a_pw1_w[i], a_pw1_b[i], a_dw_w[i], a_dw_b[i],
                                   a_ln_g[i], a_ln_b[i], a_pw2_w[i], a_pw2_b[i])
        else:
            j = layer - n_a
            h = h + nsa_attention(h, b_norm[j], b_w_in[j], b_w_out[j], kv)
        h = h + swiglu_ffn(h, ffn_norm[layer], ffn_w_in[layer], ffn_w_out[layer])
        if layer == n_a - 1:
            kv = shared_kv(h, kv_norm, w_kv, cmp_pos_k, cmp_pos_v,
                           phi_k_w1, phi_k_w2, phi_v_w1, phi_v_w2)
    return rmsnorm(h, final_norm)
```

```python
import contextlib
import numpy as np
import ml_dtypes
import concourse.bass as bass
import concourse.mybir as mybir
from concourse.bass_utils import run_bass_kernel_spmd

F32 = mybir.dt.float32
BF16 = mybir.dt.bfloat16
ALU = mybir.AluOpType
AF = mybir.ActivationFunctionType

D = 1024
S = 2048
KC = 8
FF = 2816
FC = 22
H = 16
DH = 64
NCMP = 127
BIG = float(2 ** 18)
EPS = 1e-6
FORCE = 1e9
TT = 512
NT = S // TT


class Op:
    __slots__ = ("eng", "fn", "deps", "needed", "sem", "val", "is_dma", "slot", "vc", "gi")


class Prog:
    ENGS = ("pe", "act", "dve", "pool", "sp")

    def __init__(self, nc, n_dma_sems=32):
        self.nc = nc
        self.ops = []
        self.last_w = {}
        self.readers = {}
        self.n_dma_sems = n_dma_sems
        self.dma_rr = 0
        self.dma_rr_q = {}
        self.dma_last = {}
        self.last_on = {}

    def add(self, eng, fn, reads=(), writes=(), dma=False):
        op = Op()
        op.eng = eng
        op.fn = fn
        op.needed = False
        op.is_dma = dma
        op.sem = None
        op.val = None
        op.slot = None
        op.gi = len(self.ops)
        deps = []
        for r in reads:
            w = self.last_w.get(r)
            if w is not None:
                deps.append(w)
        for w_ in writes:
            w = self.last_w.get(w_)
            if w is not None:
                deps.append(w)
            deps.extend(self.readers.get(w_, ()))
        if dma:
            nper = self.n_dma_sems // 2
            qi = 1 if eng == "pool" else 0
            rr = self.dma_rr_q.get(qi, 0)
            self.dma_rr_q[qi] = rr + 1
            slot = qi * nper + rr % nper
            op.slot = slot
            prev = self.dma_last.get(slot)
            if prev is not None:
                deps.append(prev)
            self.dma_last[slot] = op
        seen = set()
        fdeps = []
        for d in deps:
            if d is op or id(d) in seen:
                continue
            seen.add(id(d))
            if (not d.is_dma) and (not dma) and d.eng == eng and eng == "pe":
                continue
            fdeps.append(d)
        op.deps = fdeps
        for w_ in writes:
            self.last_w[w_] = op
            self.readers[w_] = []
        for r in reads:
            if r in writes:
                continue
            self.readers.setdefault(r, []).append(op)
        self.ops.append(op)
        if not dma:
            self.last_on[eng] = op
        return op

    def pe(self, fn, reads=(), writes=()):
        return self.add("pe", fn, reads, writes)

    def act(self, fn, reads=(), writes=()):
        return self.add("act", fn, reads, writes)

    def dve(self, fn, reads=(), writes=()):
        return self.add("dve", fn, reads, writes)

    def pool(self, fn, reads=(), writes=()):
        return self.add("pool", fn, reads, writes)

    def dma(self, q, fn, reads=(), writes=()):
        return self.add(q, fn, reads, writes, dma=True)

    def barrier(self):
        deps = [o for o in self.last_on.values()] + [o for o in self.dma_last.values()]
        for e in self.ENGS:
            op = Op()
            op.eng = e
            op.fn = None
            op.needed = False
            op.is_dma = False
            op.sem = None
            op.val = None
            op.slot = None
            op.gi = len(self.ops)
            op.deps = list(deps)
            self.ops.append(op)
        self.last_w = {}
        self.readers = {}

    def emit(self, final_wait_ops=()):
        nc = self.nc
        engs = {"pe": nc.tensor, "act": nc.scalar, "dve": nc.vector, "pool": nc.gpsimd, "sp": nc.sync}
        for op in self.ops:
            for d in op.deps:
                d.needed = True
        for op in final_wait_ops:
            op.needed = True
        with contextlib.ExitStack() as st:
            esem = {e: st.enter_context(nc.semaphore("s_" + e)) for e in engs}
            dsem = [st.enter_context(nc.semaphore("d_%d" % i)) for i in range(self.n_dma_sems)]
            ecount = {e: 0 for e in engs}
            dcount = [0] * self.n_dma_sems
            seen = {e: {} for e in engs}
            nwait = 0
            nfused = 0
            for op in self.ops:
                e = engs[op.eng]
                sn = seen[op.eng]
                pend = []
                for d in sorted(op.deps, key=lambda o: -o.gi):
                    if d.fn is None:
                        continue
                    assert d.val is not None
                    key = d.sem.name
                    if sn.get(key, 0) < d.val:
                        pend.append(d)
                        nwait += 1
                        sn[key] = d.val
                    for k_, v_ in d.vc.items():
                        if sn.get(k_, 0) < v_:
                            sn[k_] = v_
                fuse = op.fn is not None and len(pend) > 0
                for d in (pend[:-1] if fuse else pend):
                    e.wait_ge(d.sem, d.val)
                if op.fn is None:
                    continue
                op.vc = dict(sn)
                ins = op.fn(e)
                if fuse:
                    ins._wait_ge(pend[-1].sem, pend[-1].val)
                    nfused += 1
                if op.is_dma:
                    dcount[op.slot] += 16
                    op.sem = dsem[op.slot]
                    op.val = dcount[op.slot]
                    ins.then_inc(op.sem, 16)
                elif op.needed:
                    ecount[op.eng] += 1
                    op.sem = esem[op.eng]
                    op.val = ecount[op.eng]
                    ins.then_inc(op.sem, 1)
            for op in final_wait_ops:
                nc.sync.wait_ge(op.sem, op.val)
            self.stats = dict(n_ops=len(self.ops), n_wait=nwait, n_fused=nfused, ecount=dict(ecount))


def _bf(x):
    return np.asarray(x, np.float32).astype(ml_dtypes.bfloat16).astype(np.float32)


def make_consts():
    c = {}
    c["c_ident"] = np.eye(128, dtype=np.float32)
    kl = np.arange(128)[:, None]
    ql = np.arange(128)[None, :]
    c["c_triA"] = np.where(kl > ql, -BIG, 0.0).astype(np.float32)
    c["c_triB"] = np.where(ql >= kl, -BIG, 0.0).astype(np.float32)
    n = np.arange(128)[:, None, None]
    t = (np.arange(4)[None, :, None] * TT + np.arange(TT)[None, None, :])
    c["c_cmpmask"] = np.where(t < 16 * n + 31, -BIG, 0.0).astype(np.float32)
    key = np.arange(S)[None, :]
    j = np.arange(32)[:, None]
    c["c_E"] = np.where(key // 64 == j, -BIG, 0.0).astype(np.float32)
    slopes = np.asarray(2.0 ** (-8.0 * (np.arange(H) + 1) / H), np.float32)
    qaug = np.zeros((7, H, TT), np.float32)
    for h in range(H):
        s = np.float32(slopes[h])
        s1 = _bf(s)
        s2 = _bf(np.float32(s - s1))
        s3 = _bf(np.float32(s - s1 - s2))
        qaug[0, h, :] = _bf(-8.0 * s * np.arange(TT, dtype=np.float32))
        for r, sv in enumerate((s1, s2, s3)):
            qaug[1 + r, h, :] = 8.0 * sv
            qaug[4 + r, h, :] = 8.0 * sv
    c["c_qaug"] = qaug
    kaug = np.zeros((7, S), np.float32)
    kk = np.arange(S)
    kaug[0] = 1.0
    kaug[1:4] = (kk % 128)[None, :]
    kaug[4:7] = (kk - kk % 128)[None, :]
    c["c_kaug"] = kaug
    kc = np.zeros((7, 128), np.float32)
    kc[0] = 1.0
    kc[1:4] = (16 * np.arange(128))[None, :]
    kc[4:7] = 31.0
    c["c_kaugc"] = kc
    p = np.arange(128)[:, None, None]
    sj = np.arange(16)[None, :, None]
    jb = np.arange(32)[None, None, :]
    tq = 128 * sj + p
    tb = tq // 64
    forced = (jb == 0) | (jb == tb) | (jb == tb - 1)
    causal = jb * 64 <= tq
    c["c_cmask"] = (causal & ~forced).astype(np.float32)
    c["c_amask"] = np.where(forced, FORCE, np.where(causal, 0.0, -FORCE)).astype(np.float32)
    ab = np.zeros((128, H * 4), np.float32)
    for h in range(H):
        for cc in range(4):
            ab[:, h * 4 + cc] = -np.float32(slopes[h]) * np.float32(TT * cc)
    c["c_abias"] = ab
    cs = np.arange(128)[:, None] * 16
    ss = np.arange(32)[None, :] * 64
    ov = ((cs <= ss + 63) & (cs + 31 >= ss)).astype(np.float32)
    ov[127:] = 0.0
    c["c_ovl"] = ov
    return c


CONST_SHAPES = {
    "c_ident": [128, 128], "c_triA": [128, 128], "c_triB": [128, 128], "c_cmpmask": [128, 4, TT],
    "c_E": [32, S], "c_qaug": [7, H, TT], "c_kaug": [7, S], "c_kaugc": [7, 128],
    "c_cmask": [128, 16, 32], "c_amask": [128, 16, 32], "c_abias": [128, H * 4], "c_ovl": [128, 32],
}

V_ANORM = 0
V_PW1B = 8
V_DWW = 24
V_DWB = 24 + 248
V_LNG = V_DWB + 8
V_LNB = V_LNG + 8
V_PW2B = V_LNB + 8
V_FFN0 = V_PW2B + 8
V_KVN = V_FFN0 + 8
V_BN = V_KVN + 8
V_FFN1 = V_BN + 8
V_FIN = V_FFN1 + 8
NV = V_FIN + 8


def pack_vecs(inp):
    def pk(v):
        v = np.asarray(v, np.float32).reshape(-1, 128)
        return v.T
    cols = [pk(inp["a_norm"][0]), pk(inp["a_pw1_b"][0]),
            np.asarray(inp["a_dw_w"][0], np.float32).reshape(31, 8, 128).transpose(2, 0, 1).reshape(128, 248),
            pk(inp["a_dw_b"][0]), pk(inp["a_ln_g"][0]), pk(inp["a_ln_b"][0]), pk(inp["a_pw2_b"][0]),
            pk(inp["ffn_norm"][0]), pk(inp["kv_norm"]), pk(inp["b_norm"][0]), pk(inp["ffn_norm"][1]),
            pk(inp["final_norm"])]
    out = np.ascontiguousarray(np.concatenate(cols, axis=1), dtype=np.float32)
    assert out.shape == (128, NV)
    return out


def build(stop_after=None, dbg_kv=False):
    nc = bass.Bass("TRN2", target_bir_lowering=False)

    def din(name, shape):
        return nc.dram_tensor(name, list(shape), F32, kind="ExternalInput").ap()

    xT = din("xT", [D, S])
    vecs_d = din("vecs", [128, NV])
    w_pw1 = din("a_pw1_w", [D, 2 * D])
    w_pw2 = din("a_pw2_w", [D, D])
    w_kv = din("w_kv", [D, 768])
    posk = din("posTk", [64, 32])
    posv = din("posTv", [64, 32])
    w1k = din("phi_k_w1", [2048, 64])
    w2k = din("phi_k_w2", [64, 64])
    w1v = din("phi_v_w1", [2048, 64])
    w2v = din("phi_v_w2", [64, 64])
    w_bin = din("b_w_in", [D, 1072])
    w_bout = din("b_w_out", [D, D])
    w_fin = [din("ffn_w_in0", [D, 2 * FF]), din("ffn_w_in1", [D, 2 * FF])]
    w_fout = [din("ffn_w_out0", [FF, D]), din("ffn_w_out1", [FF, D])]
    cst = {k: din(k, v) for k, v in CONST_SHAPES.items()}
    outT = nc.dram_tensor("outT", [D, S], F32, kind="ExternalOutput").ap()
    dbg = {}
    if dbg_kv:
        for nm, shp in (("d_kslc", [2, 128, S]), ("d_kwin", [2, 128, S]), ("d_kcmp", [2, 128, 128]),
                        ("d_vslc", [128, 2 * 16 * 65]), ("d_vwin", [128, 2 * 16 * 65]), ("d_vcmp", [2, 128, 97])):
            dbg[nm] = nc.dram_tensor(nm, shp, F32, kind="ExternalOutput").ap()

    P = Prog(nc)
    final_ops = []
    with contextlib.ExitStack() as g_st:
        name_ctr = {}

        def sbuf(st, name, shape, dt):
            n = name_ctr.get(name, 0)
            name_ctr[name] = n + 1
            if n:
                name = "%s_r%d" % (name, n)
            return st.enter_context(nc.sbuf_tensor(name, list(shape), dt))

        hT = sbuf(g_st, "hT", [128, KC, S], F32)
        vecs = sbuf(g_st, "vecs_sb", [128, NV], F32)
        ident = sbuf(g_st, "ident", [128, 128], BF16)
        ones = sbuf(g_st, "ones", [128, 128], BF16)
        epsc = sbuf(g_st, "epsc", [128, 1], F32)
        zer = sbuf(g_st, "zer", [128, 260], BF16)
        rstd = sbuf(g_st, "rstd", [128, TT], F32)
        sq = sbuf(g_st, "sq", [128, KC, TT], BF16)
        ps = [g_st.enter_context(nc.psum_tensor("ps%d" % i, [128, 512], F32)) for i in range(7)]
        pst = g_st.enter_context(nc.psum_tensor("pst", [128, 1024], BF16))

        P.dve(lambda e: e.memset(ones[:], 1.0), writes=["ones"])
        P.dve(lambda e: e.memset(epsc[:], EPS), writes=["epsc"])
        P.dve(lambda e: e.memset(zer[:], 0.0), writes=["zer"])
        P.dma("sp", lambda e: e.dma_start(out=vecs[:], in_=vecs_d), writes=["vecs"])
        P.dma("pool", lambda e: e.dma_start(out=ident[:], in_=cst["c_ident"]), writes=["ident"])
        xT_v = xT.rearrange("(k p) t -> p k t", p=128)
        for t in range(NT):
            for k in range(KC):
                P.dma("sp", lambda e, k=k, t=t: e.dma_start(out=hT[:, k, t * TT:(t + 1) * TT], in_=xT_v[:, k, t * TT:(t + 1) * TT]),
                      writes=[("h", k, t)])

        def hkeys(k, t0, T):
            return [("h", k, t) for t in range(t0 // TT, (t0 + T - 1) // TT + 1)]

        def norm_tile(t0, gcol, xb, xoff, xkey, stat_ps=6, sq_pool=False, defer_xb=False, part=None):
            pst_ = ps[stat_ps]
            for k in (range(KC) if part in (None, "sq") else ()):
                if sq_pool and k % 2 == 1:
                    P.pool(lambda e, k=k: e.tensor_tensor(out=sq[:, k, :], in0=hT[:, k, t0:t0 + TT], in1=hT[:, k, t0:t0 + TT], op=ALU.mult),
                           reads=hkeys(k, t0, TT), writes=[("sq", k)])
                else:
                    P.act(lambda e, k=k: e.activation(out=sq[:, k, :], in_=hT[:, k, t0:t0 + TT], func=AF.Square),
                          reads=hkeys(k, t0, TT), writes=[("sq", k)])
            if part == "sq":
                return None
            for k in range(KC):
                P.pe(lambda e, k=k: e.matmul(pst_[:, :], lhsT=ones[:], rhs=sq[:, k, :], start=(k == 0), stop=(k == KC - 1)),
                     reads=["ones", ("sq", k)], writes=[("ps", stat_ps)])
            P.act(lambda e: e.activation(out=rstd[:], in_=pst_[:, :], func=AF.Ln, scale=1.0 / D, bias=epsc[:]),
                  reads=[("ps", stat_ps), "epsc"], writes=["rstd"])
            P.act(lambda e: e.activation(out=rstd[:], in_=rstd[:], func=AF.Exp, scale=-0.5), reads=["rstd"], writes=["rstd"])
            def xb_part():
                for k in range(KC):
                    P.dve(lambda e, k=k: e.scalar_tensor_tensor(out=xb[:, k, xoff:xoff + TT], in0=hT[:, k, t0:t0 + TT],
                                                                scalar=vecs[:, gcol + k:gcol + k + 1], in1=rstd[:],
                                                                op0=ALU.mult, op1=ALU.mult),
                          reads=hkeys(k, t0, TT) + ["rstd", "vecs"], writes=[(xkey, k, xoff // TT)])
            if defer_xb:
                return xb_part
            xb_part()
            return None

        def load_w(dst, src, rows_k, key, q="pool", col0=0, ncols=None):
            sv = src.rearrange("(k p) n -> p k n", p=128)
            if ncols is None:
                ncols = src.shape[1]
            step = 4 if ncols <= 512 else 1
            for k0 in range(0, rows_k, step):
                k1 = min(rows_k, k0 + step)
                P.dma(q, lambda e, k0=k0, k1=k1: e.dma_start(out=dst[:, k0:k1, 0:ncols], in_=sv[:, k0:k1, col0:col0 + ncols]),
                      writes=[(key, k) for k in range(k0, k1)])

        def phase_A():
            with contextlib.ExitStack() as st:
                hglu = sbuf(st, "hglu", [128, KC, 30 + S], BF16)
                diagA = sbuf(st, "diagA", [128, 31 * 4, 128], BF16)
                dstate = {"B": None, "n": 0}

                def dg(j, m):
                    t_ = diagA if m < 4 else dstate["B"]
                    return t_[:, j * 4 + (m % 4), :]

                def build_diag_one(j, m):
                    jm = j * 8 + m
                    dstate["n"] += 1
                    if dstate["n"] % 2 == 0:
                        P.dve(lambda e: e.tensor_scalar(out=dg(j, m), in0=ident[:, :], scalar1=vecs[:, V_DWW + jm:V_DWW + jm + 1],
                                                        scalar2=None, op0=ALU.mult),
                              reads=["ident", "vecs"], writes=[("diag", jm)])
                    else:
                        P.act(lambda e: e.activation(out=dg(j, m), in_=ident[:, :], func=AF.Copy,
                                                     scale=vecs[:, V_DWW + jm:V_DWW + jm + 1]),
                              reads=["ident", "vecs"], writes=[("diag", jm)])
                early_list = [(j, m) for m in range(4) for j in range(31)]
                with contextlib.ExitStack() as st1:
                    wA1 = sbuf(st1, "wA1", [128, KC, 2 * D], BF16)
                    xbs = [sbuf(st1, "xbA%d" % i, [128, KC, TT], BF16) for i in range(2)]
                    sig = [sbuf(st1, "sig%d" % i, [128, TT], F32) for i in range(2)]
                    sv1 = w_pw1.rearrange("(k p) n -> p k n", p=128)
                    for cb in (0, 2, 1, 3):
                        for k0 in (0, 4):
                            P.dma("pool", lambda e, cb=cb, k0=k0: e.dma_start(out=wA1[:, k0:k0 + 4, cb * 512:(cb + 1) * 512],
                                                                              in_=sv1[:, k0:k0 + 4, cb * 512:(cb + 1) * 512]),
                                  writes=[("wA1", cb, k0)])
                    P.dve(lambda e: e.memset(hglu[:, :, 0:30], 0.0), writes=[("hg", k, -1) for k in range(KC)])
                    norm_tile(0, V_ANORM, xbs[0], 0, ("xb", 0))
                    for t in range(NT):
                        t0 = t * TT
                        xb = xbs[t % 2]
                        xk = ("xb", t % 2)
                        if t + 1 < NT:
                            norm_tile(t0 + TT, V_ANORM, xbs[(t + 1) % 2], 0, ("xb", (t + 1) % 2))
                        for m in range(KC):
                            pa, pb = ps[m % 2], ps[2 + m % 2]
                            for k in range(KC):
                                P.pe(lambda e, k=k, m=m, pa=pa, xb=xb: e.matmul(pa[:, :], lhsT=wA1[:, k, m * 128:(m + 1) * 128], rhs=xb[:, k, :],
                                                                                 start=(k == 0), stop=(k == KC - 1)),
                                     reads=[("wA1", m // 4, (k // 4) * 4), (xk, k, 0)], writes=[("ps", m % 2)])
                            for k in range(KC):
                                P.pe(lambda e, k=k, m=m, pb=pb, xb=xb: e.matmul(pb[:, :], lhsT=wA1[:, k, D + m * 128:D + (m + 1) * 128], rhs=xb[:, k, :],
                                                                                 start=(k == 0), stop=(k == KC - 1)),
                                     reads=[("wA1", 2 + m // 4, (k // 4) * 4), (xk, k, 0)], writes=[("ps", 2 + m % 2)])
                            sg = sig[m % 2]
                            P.act(lambda e, m=m, pb=pb, sg=sg: e.activation(out=sg[:], in_=pb[:, :], func=AF.Sigmoid,
                                                                             bias=vecs[:, V_PW1B + 8 + m:V_PW1B + 9 + m]),
                                  reads=[("ps", 2 + m % 2), "vecs"], writes=[("sig", m % 2)])
                            P.dve(lambda e, m=m, pa=pa, sg=sg, t0=t0: e.scalar_tensor_tensor(out=hglu[:, m, 30 + t0:30 + t0 + TT], in0=pa[:, :],
                                                                                             scalar=vecs[:, V_PW1B + m:V_PW1B + m + 1], in1=sg[:],
                                                                                             op0=ALU.add, op1=ALU.mult),
                                  reads=[("ps", m % 2), ("sig", m % 2), "vecs"], writes=[("hg", m, t)])
                            for _ in range(4):
                                if early_list:
                                    build_diag_one(*early_list.pop(0))
                P.barrier()
                with contextlib.ExitStack() as st2:
                    dstate["B"] = sbuf(st2, "diagB", [128, 31 * 4, 128], BF16)
                    wA2 = sbuf(st2, "wA2", [128, KC, D], BF16)
                    cz = sbuf(st2, "cz", [128, KC, TT], BF16)
                    tmp = [sbuf(st2, "lntmp%d" % i, [128, TT], F32) for i in range(2)]
                    mean = sbuf(st2, "mean", [128, TT], F32)
                    msq = sbuf(st2, "msq", [128, TT], F32)
                    rs2 = sbuf(st2, "rs2", [128, TT], F32)
                    load_w(wA2, w_pw2, KC, "wA2")
                    while early_list:
                        build_diag_one(*early_list.pop(0))
                    for m in range(4, KC):
                        for j in range(31):
                            build_diag_one(j, m)
                    cstate = {"cnt": 0}
                    pending = {}

                    def conv_mm(t, m):
                        t0 = t * TT
                        bank = cstate["cnt"] % 4
                        cstate["cnt"] += 1
                        pc = ps[bank]
                        kc_ = ("ps", bank)
                        for j in range(31):
                            lo = t0 + j
                            rk = [("hg", m, tt_) for tt_ in range((lo - 30) // TT if lo >= 30 else -1, (lo + TT - 1 - 30) // TT + 1)]
                            P.pe(lambda e, j=j, lo=lo: e.matmul(pc[:, :], lhsT=dg(j, m), rhs=hglu[:, m, lo:lo + TT],
                                                                start=(j == 0), stop=(j == 30)),
                                 reads=[("diag", j * 8 + m)] + rk, writes=[kc_])
                        pending[(t, m)] = bank

                    def conv_evac(t, m):
                        bank = pending.pop((t, m))
                        pc = ps[bank]
                        kc_ = ("ps", bank)
                        if m % 2 == 0:
                            P.act(lambda e: e.activation(out=cz[:, m, :], in_=pc[:, :], func=AF.Identity,
                                                         bias=vecs[:, V_DWB + m:V_DWB + m + 1]),
                                  reads=[kc_, "vecs"], writes=[("cz", m)])
                        else:
                            P.dve(lambda e: e.tensor_scalar(out=cz[:, m, :], in0=pc[:, :], scalar1=vecs[:, V_DWB + m:V_DWB + m + 1],
                                                            scalar2=None, op0=ALU.add),
                                  reads=[kc_, "vecs"], writes=[("cz", m)])

                    for m in range(KC):
                        conv_mm(0, m)
                        conv_evac(0, m)
                    for t in range(NT):
                        t0 = t * TT
                        for m in range(KC):
                            P.act(lambda e, m=m: e.activation(out=sq[:, m, :], in_=cz[:, m, :], func=AF.Square),
                                  reads=[("cz", m)], writes=[("sq", m)])
                        for m in range(KC):
                            P.pe(lambda e, m=m: e.matmul(ps[4][:, :], lhsT=ones[:], rhs=cz[:, m, :], start=(m == 0), stop=(m == KC - 1)),
                                 reads=["ones", ("cz", m)], writes=[("ps", 4)])
                        for m in range(KC):
                            P.pe(lambda e, m=m: e.matmul(ps[5][:, :], lhsT=ones[:], rhs=sq[:, m, :], start=(m == 0), stop=(m == KC - 1)),
                                 reads=["ones", ("sq", m)], writes=[("ps", 5)])
                        if t + 1 < NT:
                            for m in range(4):
                                conv_mm(t + 1, m)
                        P.dve(lambda e: e.tensor_scalar(out=mean[:], in0=ps[4][:, :], scalar1=1.0 / D, scalar2=None, op0=ALU.mult),
                              reads=[("ps", 4)], writes=["mean"])
                        P.dve(lambda e: e.tensor_tensor(out=msq[:], in0=mean[:], in1=mean[:], op=ALU.mult), reads=["mean"], writes=["msq"])
                        P.dve(lambda e: e.scalar_tensor_tensor(out=msq[:], in0=ps[5][:, :], scalar=1.0 / D, in1=msq[:],
                                                               op0=ALU.mult, op1=ALU.subtract),
                              reads=[("ps", 5), "msq"], writes=["msq"])
                        P.act(lambda e: e.activation(out=rs2[:], in_=msq[:], func=AF.Ln, bias=epsc[:]),
                              reads=["msq", "epsc"], writes=["rs2"])
                        P.act(lambda e: e.activation(out=rs2[:], in_=rs2[:], func=AF.Exp, scale=-0.5), reads=["rs2"], writes=["rs2"])
                        for m in range(KC):
                            tm = tmp[m % 2]
                            P.dve(lambda e, m=m, tm=tm: e.tensor_tensor(out=tm[:], in0=cz[:, m, :], in1=mean[:], op=ALU.subtract),
                                  reads=[("cz", m), "mean"], writes=[("lntmp", m % 2)])
                            P.dve(lambda e, m=m, tm=tm: e.tensor_tensor(out=tm[:], in0=tm[:], in1=rs2[:], op=ALU.mult),
                                  reads=[("lntmp", m % 2), "rs2"], writes=[("lntmp", m % 2)])
                            P.act(lambda e, m=m, tm=tm: e.activation(out=cz[:, m, :], in_=tm[:], func=AF.Silu,
                                                                     scale=vecs[:, V_LNG + m:V_LNG + m + 1], bias=vecs[:, V_LNB + m:V_LNB + m + 1]),
                                  reads=[("lntmp", m % 2), "vecs"], writes=[("cz", m)])
                        for mo in range(KC):
                            po = ps[4 + mo % 2]
                            kpo = ("ps", 4 + mo % 2)
                            for m in range(KC):
                                P.pe(lambda e, m=m, mo=mo, po=po: e.matmul(po[:, :], lhsT=wA2[:, m, mo * 128:(mo + 1) * 128], rhs=cz[:, m, :],
                                                                            start=(m == 0), stop=(m == KC - 1)),
                                     reads=[("wA2", m), ("cz", m)], writes=[kpo])
                            P.dve(lambda e, mo=mo, po=po, t0=t0: e.scalar_tensor_tensor(out=hT[:, mo, t0:t0 + TT], in0=po[:, :],
                                                                                        scalar=vecs[:, V_PW2B + mo:V_PW2B + mo + 1],
                                                                                        in1=hT[:, mo, t0:t0 + TT], op0=ALU.add, op1=ALU.add),
                                  reads=[kpo, "vecs"] + hkeys(mo, t0, TT), writes=hkeys(mo, t0, TT))
                        if t + 1 < NT:
                            for m in range(4):
                                conv_evac(t + 1, m)
                            for m in range(4, KC):
                                conv_mm(t + 1, m)
                                conv_evac(t + 1, m)
            P.barrier()

        ob_state = {"i": 0}

        def final_tile(t0, obf):
            ov = outT.rearrange("(k p) t -> p k t", p=128)
            for k in range(KC):
                P.act(lambda e, k=k: e.activation(out=sq[:, k, :], in_=hT[:, k, t0:t0 + TT], func=AF.Square),
                      reads=hkeys(k, t0, TT), writes=[("sq", k)])
            for k in range(KC):
                P.pe(lambda e, k=k: e.matmul(ps[6][:, :], lhsT=ones[:], rhs=sq[:, k, :], start=(k == 0), stop=(k == KC - 1)),
                     reads=["ones", ("sq", k)], writes=[("ps", 6)])
            P.act(lambda e: e.activation(out=rstd[:], in_=ps[6][:, :], func=AF.Ln, scale=1.0 / D, bias=epsc[:]),
                  reads=[("ps", 6), "epsc"], writes=["rstd"])
            P.act(lambda e: e.activation(out=rstd[:], in_=rstd[:], func=AF.Exp, scale=-0.5), reads=["rstd"], writes=["rstd"])
            for k in range(KC):
                i = ob_state["i"] % len(obf)
                ob_state["i"] += 1
                o = obf[i]
                P.dve(lambda e, k=k, o=o: e.scalar_tensor_tensor(out=o[:, :], in0=hT[:, k, t0:t0 + TT],
                                                                 scalar=vecs[:, V_FIN + k:V_FIN + k + 1], in1=rstd[:],
                                                                 op0=ALU.mult, op1=ALU.mult),
                      reads=hkeys(k, t0, TT) + ["rstd", "vecs"], writes=[("obf", i)])
                final_ops.append(P.dma("sp", lambda e, k=k, o=o: e.dma_start(out=ov[:, k, t0:t0 + TT], in_=o[:, :]),
                                       reads=[("obf", i)]))

        def phase_ffn(l, final=False):
            gcol = V_FFN0 if l == 0 else V_FFN1
            win_d, wout_d = w_fin[l], w_fout[l]
            HB = 1024
            with contextlib.ExitStack() as st:
                xb = sbuf(st, "xbF", [128, KC, HB], BF16)
                actT = sbuf(st, "actT", [128, FC, HB], BF16)
                NB = 3
                wg = [sbuf(st, "wg%d" % i, [128, KC, 256], BF16) for i in range(NB)]
                wu = [sbuf(st, "wu%d" % i, [128, KC, 256], BF16) for i in range(NB)]
                wo = [sbuf(st, "wo%d" % i, [128, FC, 256], BF16) for i in range(2)]
                sgt = [sbuf(st, "sgt%d" % i, [128, TT], F32) for i in range(2)]
                obf = [sbuf(st, "obf%d" % i, [128, TT], F32) for i in range(4)] if final else None
                cnt = 0
                for hb in range(S // HB):
                    if hb == 0:
                        for tt in range(HB // TT):
                            norm_tile(hb * HB + tt * TT, gcol, xb, tt * TT, "xbF")
                    for blk in range(FF // 256):
                        b = blk % NB
                        load_w(wg[b], win_d, KC, ("wg", b), col0=blk * 256, ncols=256)
                        load_w(wu[b], win_d, KC, ("wu", b), col0=FF + blk * 256, ncols=256)
                        for fi in range(2):
                            f = blk * 2 + fi
                            for tt in range(HB // TT):
                                pg, pu = ps[cnt % 2], ps[2 + cnt % 2]
                                kg, ku = ("ps", cnt % 2), ("ps", 2 + cnt % 2)
                                for k in range(KC):
                                    P.pe(lambda e, k=k, b=b, fi=fi, tt=tt, pg=pg: e.matmul(pg[:, :], lhsT=wg[b][:, k, fi * 128:(fi + 1) * 128],
                                                                                           rhs=xb[:, k, tt * TT:(tt + 1) * TT],
                                                                                           start=(k == 0), stop=(k == KC - 1)),
                                         reads=[(("wg", b), k), ("xbF", k, tt)], writes=[kg])
                                for k in range(KC):
                                    P.pe(lambda e, k=k, b=b, fi=fi, tt=tt, pu=pu: e.matmul(pu[:, :], lhsT=wu[b][:, k, fi * 128:(fi + 1) * 128],
                                                                                           rhs=xb[:, k, tt * TT:(tt + 1) * TT],
                                                                                           start=(k == 0), stop=(k == KC - 1)),
                                         reads=[(("wu", b), k), ("xbF", k, tt)], writes=[ku])
                                sg = sgt[cnt % 2]
                                P.act(lambda e, pg=pg, sg=sg: e.activation(out=sg[:], in_=pg[:, :], func=AF.Silu),
                                      reads=[kg], writes=[("sgt", cnt % 2)])
                                P.dve(lambda e, f=f, tt=tt, pu=pu, sg=sg: e.tensor_tensor(out=actT[:, f, tt * TT:(tt + 1) * TT], in0=sg[:], in1=pu[:, :],
                                                                                           op=ALU.mult),
                                      reads=[ku, ("sgt", cnt % 2)], writes=[("actT", f, tt)])
                                cnt += 1
                    for ob in range(D // 256):
                        if ob == 1 and hb + 1 < S // HB:
                            for tt in range(HB // TT):
                                norm_tile((hb + 1) * HB + tt * TT, gcol, xb, tt * TT, "xbF")
                        b = ob % 2
                        sv = wout_d.rearrange("(f p) n -> p f n", p=128)
                        for f0 in range(0, FC, 6):
                            f1 = min(FC, f0 + 6)
                            P.dma("pool", lambda e, b=b, ob=ob, f0=f0, f1=f1: e.dma_start(out=wo[b][:, f0:f1, :], in_=sv[:, f0:f1, ob * 256:(ob + 1) * 256]),
                                  writes=[(("wo", b), f) for f in range(f0, f1)])
                        for mi in range(2):
                            m = ob * 2 + mi
                            for tt in range(HB // TT):
                                po = ps[4 + cnt % 2]
                                kp = ("ps", 4 + cnt % 2)
                                t0 = hb * HB + tt * TT
                                for f in range(FC):
                                    P.pe(lambda e, f=f, b=b, mi=mi, tt=tt, po=po: e.matmul(po[:, :], lhsT=wo[b][:, f, mi * 128:(mi + 1) * 128],
                                                                                           rhs=actT[:, f, tt * TT:(tt + 1) * TT],
                                                                                           start=(f == 0), stop=(f == FC - 1)),
                                         reads=[(("wo", b), f), ("actT", f, tt)], writes=[kp])
                                P.dve(lambda e, m=m, t0=t0, po=po: e.tensor_tensor(out=hT[:, m, t0:t0 + TT], in0=hT[:, m, t0:t0 + TT], in1=po[:, :],
                                                                                    op=ALU.add),
                                      reads=[kp] + hkeys(m, t0, TT), writes=hkeys(m, t0, TT))
                                cnt += 1
                    if final:
                        for tt in range(HB // TT):
                            final_tile(hb * HB + tt * TT, obf)
            P.barrier()

        def phase_CD(do_D=True):
            with contextlib.ExitStack() as st:
                kslc = [sbuf(st, "kslc%d" % g, [128, S], BF16) for g in range(2)]
                kwin = [sbuf(st, "kwin%d" % g, [128, S], BF16) for g in range(2)]
                kcmp = [sbuf(st, "kcmp%d" % g, [128, 128], BF16) for g in range(2)]
                vslc = sbuf(st, "vslc", [128, 2, 16, 65], BF16)
                vwin = sbuf(st, "vwin", [128, 2, 16, 65], BF16)
                vcmp = [sbuf(st, "vcmp%d" % g, [128, 97], BF16) for g in range(2)]
                for g in range(2):
                    P.dve(lambda e, g=g: e.memset(kslc[g][64:128, :], 0.0), writes=[("kslc_aug", g)])
                    P.dma("pool", lambda e, g=g: e.dma_start(out=kslc[g][64:71, :], in_=cst["c_kaug"]), writes=[("kslc_aug", g)])
                    P.dma("pool", lambda e, g=g: e.dma_start(out=kslc[g][96:128, :], in_=cst["c_E"]), writes=[("kslc_aug", g)])
                    P.dma("pool", lambda e, g=g: e.dma_start(out=kwin[g][64:71, :], in_=cst["c_kaug"]), writes=[("kwin_aug", g)])
                    P.dma("pool", lambda e, g=g: e.dma_start(out=kcmp[g][64:71, :], in_=cst["c_kaugc"]), writes=[("kcmp_aug", g)])
                    P.dma("pool", lambda e, g=g: e.dma_start(out=vcmp[g][:, 65:97], in_=cst["c_ovl"]), writes=[("vcmp_c", g)])
                    P.dve(lambda e, g=g: e.memset(vcmp[g][:, 64:65], 1.0), writes=[("vcmp_1", g)])
                P.dve(lambda e: e.memset(vslc[:, :, :, 64:65], 1.0), writes=["vslc_1"])
                P.dve(lambda e: e.memset(vwin[:, :, :, 64:65], 1.0), writes=["vwin_1"])

                pre = phase_D_prefetch(st) if do_D else None
                with contextlib.ExitStack() as st2:
                    wkv = sbuf(st2, "wkv", [128, KC, 768], BF16)
                    xbs = [sbuf(st2, "xbC%d" % i, [128, KC, TT], BF16) for i in range(2)]
                    rawT = sbuf(st2, "rawT", [128, 2, S], BF16)
                    w1kb = sbuf(st2, "w1kb", [128, 32, 64], BF16)
                    w1vb = sbuf(st2, "w1vb", [128, 32, 64], BF16)
                    w2kb = sbuf(st2, "w2kb", [64, 64], BF16)
                    w2vb = sbuf(st2, "w2vb", [64, 64], BF16)
                    pkb = sbuf(st2, "pkb", [64, 32], BF16)
                    pvb = sbuf(st2, "pvb", [64, 32], BF16)
                    cbias = sbuf(st2, "cbias", [64, 2], F32)
                    acmp = sbuf(st2, "acmp", [64, 4, 128], BF16)
                    load_w(wkv, w_kv, KC, "wkv")
                    for hf in range(2):
                        P.dma("pool", lambda e, hf=hf: e.dma_start(out=w1kb[hf * 64:(hf + 1) * 64, :, :], in_=w1k.rearrange("(l d) o -> d l o", d=64)),
                              writes=[("w1kb", hf)])
                        P.dma("pool", lambda e, hf=hf: e.dma_start(out=w1vb[hf * 64:(hf + 1) * 64, :, :], in_=w1v.rearrange("(l d) o -> d l o", d=64)),
                              writes=[("w1vb", hf)])
                    P.dma("pool", lambda e: e.dma_start(out=w2kb[:], in_=w2k), writes=["w2kb"])
                    P.dma("pool", lambda e: e.dma_start(out=w2vb[:], in_=w2v), writes=["w2vb"])
                    P.dma("pool", lambda e: e.dma_start(out=pkb[:], in_=posk), writes=["pkb"])
                    P.dma("pool", lambda e: e.dma_start(out=pvb[:], in_=posv), writes=["pvb"])
                    if pre is not None:
                        pre["loads"]()
                    cnt = 0
                    norm_tile(0, V_KVN, xbs[0], 0, ("xbC", 0))
                    for t in range(NT):
                        t0 = t * TT
                        xb = xbs[t % 2]
                        xk = ("xbC", t % 2)
                        if t + 1 < NT:
                            norm_tile(t0 + TT, V_KVN, xbs[(t + 1) % 2], 0, ("xbC", (t + 1) % 2), part="sq")
                        for part in (0, 1, 2, 4):
                            pp = ps[cnt % 4]
                            kp = ("ps", cnt % 4)
                            cnt += 1
                            for k in range(KC):
                                P.pe(lambda e, k=k, part=part, pp=pp, xb=xb: e.matmul(pp[:, :], lhsT=wkv[:, k, part * 128:(part + 1) * 128], rhs=xb[:, k, :],
                                                                                       start=(k == 0), stop=(k == KC - 1)),
                                     reads=[("wkv", k), (xk, k, 0)], writes=[kp])
                            if part in (0, 1):
                                P.act(lambda e, part=part, pp=pp, t0=t0: e.activation(out=rawT[:, part, t0:t0 + TT], in_=pp[:, :], func=AF.Copy),
                                      reads=[kp], writes=[("rawT", part * 2, t), ("rawT", part * 2 + 1, t)])
                                continue
                            for g in range(2):
                                if part == 2:
                                    dst, dk = kslc[g][0:64, t0:t0 + TT], ("kslc", g, t)
                                else:
                                    dst, dk = kwin[g][0:64, t0:t0 + TT], ("kwin", g, t)
                                P.act(lambda e, g=g, pp=pp, dst=dst: e.activation(out=dst, in_=pp[g * 64:(g + 1) * 64, :], func=AF.Copy),
                                      reads=[kp], writes=[dk])
                        if t + 1 < NT:
                            norm_tile(t0 + TT, V_KVN, xbs[(t + 1) % 2], 0, ("xbC", (t + 1) % 2), part="rest")
                        for j in range(4):
                            kt = t * 4 + j
                            pp = ps[cnt % 4]
                            kp = ("ps", cnt % 4)
                            cnt += 1
                            for ci, c0 in enumerate((384, 640)):
                                for k in range(KC):
                                    P.pe(lambda e, k=k, j=j, ci=ci, c0=c0, pp=pp, xb=xb: e.matmul(pp[:, ci * 128:(ci + 1) * 128], lhsT=xb[:, k, j * 128:(j + 1) * 128],
                                                                                                  rhs=wkv[:, k, c0:c0 + 128], start=(k == 0), stop=(k == KC - 1)),
                                         reads=[("wkv", k), (xk, k, 0)], writes=[kp])
                            P.dve(lambda e, kt=kt, pp=pp: e.tensor_copy(out=vslc[:, :, kt, 0:64], in_=pp[:, 0:128].rearrange("p (g d) -> p g d", g=2)),
                                  reads=[kp], writes=[("vslc", kt)])
                            P.dve(lambda e, kt=kt, pp=pp: e.tensor_copy(out=vwin[:, :, kt, 0:64], in_=pp[:, 128:256].rearrange("p (g d) -> p g d", g=2)),
                                  reads=[kp], writes=[("vwin", kt)])
                    raw_keys = [("rawT", i, t) for i in range(4) for t in range(NT)]
                    for ci, (w1b, pb_) in enumerate(((w1kb, pkb), (w1vb, pvb))):
                        for l in range(32):
                            P.pe(lambda e, l=l, ci=ci, w1b=w1b, pb_=pb_: e.matmul(ps[4][0:64, ci:ci + 1], lhsT=w1b[0:64, l, :], rhs=pb_[:, l:l + 1],
                                                                                  start=(l == 0), stop=(l == 31)),
                                 reads=[("w1kb", 0), ("w1vb", 0), "pkb", "pvb"], writes=[("ps", 4)])
                        P.dve(lambda e, ci=ci: e.tensor_copy(out=cbias[:, ci:ci + 1], in_=ps[4][0:64, ci:ci + 1]),
                              reads=[("ps", 4)], writes=[("cbias", ci)])
                    rv = rawT[:, :, :].rearrange("p i (n s) -> p i n s", s=16)
                    for idx in range(4):
                        ci, g = idx // 2, idx % 2
                        w1b = w1kb if ci == 0 else w1vb
                        pp = ps[idx % 4]
                        kp = ("ps", idx % 4)
                        for l in range(32):
                            P.pe(lambda e, l=l, ci=ci, g=g, w1b=w1b, pp=pp: e.matmul(pp[0:64, 0:NCMP], lhsT=w1b[g * 64:(g + 1) * 64, l, :],
                                                                                     rhs=rv[g * 64:(g + 1) * 64, ci, l // 16:l // 16 + NCMP, l % 16],
                                                                                     start=(l == 0), stop=(l == 31)),
                                 reads=raw_keys + [("w1kb", 0), ("w1kb", 1), ("w1vb", 0), ("w1vb", 1)], writes=[kp])
                        P.act(lambda e, idx=idx, ci=ci, pp=pp: e.activation(out=acmp[:, idx, 0:NCMP], in_=pp[0:64, 0:NCMP], func=AF.Silu,
                                                                            bias=cbias[:, ci:ci + 1]),
                              reads=[kp, ("cbias", ci)], writes=[("acmp", idx)])
                    for g in range(2):
                        pp = ps[4 + g]
                        kp = ("ps", 4 + g)
                        P.pe(lambda e, g=g, pp=pp: e.matmul(pp[0:64, 0:NCMP], lhsT=w2kb[:, :], rhs=acmp[:, g, 0:NCMP], start=True, stop=True),
                             reads=["w2kb", ("acmp", g)], writes=[kp])
                        P.act(lambda e, g=g, pp=pp: e.activation(out=kcmp[g][0:64, 0:NCMP], in_=pp[0:64, 0:NCMP], func=AF.Copy),
                              reads=[kp], writes=[("kcmp", g)])
                        P.pe(lambda e, g=g, pp=pp: e.matmul(pp[0:NCMP, 128:192], lhsT=acmp[:, 2 + g, 0:NCMP], rhs=w2vb[:, :], start=True, stop=True),
                             reads=["w2vb", ("acmp", 2 + g), ("kcmp", g)], writes=[kp])
                        P.dve(lambda e, g=g, pp=pp: e.tensor_copy(out=vcmp[g][0:NCMP, 0:64], in_=pp[0:NCMP, 128:192]),
                              reads=[kp], writes=[("vcmp", g)])
                P.barrier()
                if dbg_kv:
                    with contextlib.ExitStack() as st3:
                        stg = sbuf(st3, "stg", [128, S], F32)
                        stg2 = sbuf(st3, "stg2", [128, 2 * 16 * 65], F32)

                        def dump(src_ap, dst_ap, stage):
                            P.dve(lambda e: e.tensor_copy(out=stage, in_=src_ap), reads=["stg"], writes=["stg"])
                            final_ops.append(P.dma("sp", lambda e: e.dma_start(out=dst_ap, in_=stage), reads=["stg"], writes=["stgd"]))
                            P.barrier()
                        for g in range(2):
                            dump(kslc[g][:, :], dbg["d_kslc"][g], stg[:, :])
                            dump(kwin[g][:, :], dbg["d_kwin"][g], stg[:, :])
                            dump(kcmp[g][:, :], dbg["d_kcmp"][g], stg[:, 0:128])
                            dump(vcmp[g][:, :], dbg["d_vcmp"][g], stg[:, 0:97])
                        dump(vslc[:, :, :, :].rearrange("p g k d -> p (g k d)"), dbg["d_vslc"], stg2[:, :])
                        dump(vwin[:, :, :, :].rearrange("p g k d -> p (g k d)"), dbg["d_vwin"], stg2[:, :])
                if do_D:
                    phase_D(st, kslc, kwin, kcmp, vslc, vwin, vcmp, pre)
            P.barrier()

        def phase_D_prefetch(st):
            pre = {}
            pre["wq"] = wq = sbuf(st, "wq", [128, KC, 1072], BF16)
            pre["wo"] = wo = sbuf(st, "woD", [128, KC, D], BF16)
            pre["qT"] = qT = sbuf(st, "qT", [128, H, TT], BF16)
            pre["triA"] = triA = sbuf(st, "triA", [128, 128], BF16)
            pre["triB"] = triB = sbuf(st, "triB", [128, 128], BF16)
            pre["cmpm"] = cmpm = sbuf(st, "cmpm", [128, 4, TT], BF16)
            pre["cmask"] = cmask = sbuf(st, "cmask", [128, 16, 32], F32)
            pre["amask"] = amask = sbuf(st, "amask", [128, 16, 32], F32)
            pre["abias"] = abias = sbuf(st, "abias", [128, H * 4], F32)
            def loads():
                load_w(wq, w_bin, KC, "wq")
                load_w(wo, w_bout, KC, "woD")
                P.dve(lambda e: e.memset(qT[64:128, :, :], 0.0), writes=["qT_aug"] + [("qsel", g_, j_) for g_ in range(2) for j_ in range(4)])
                P.dma("pool", lambda e: e.dma_start(out=qT[64:71, :, :], in_=cst["c_qaug"]), writes=["qT_aug"])
                P.dma("pool", lambda e: e.dma_start(out=triA[:], in_=cst["c_triA"]), writes=["triA"])
                P.dma("pool", lambda e: e.dma_start(out=triB[:], in_=cst["c_triB"]), writes=["triB"])
                for c in range(4):
                    P.dma("pool", lambda e, c=c: e.dma_start(out=cmpm[:, c, :], in_=cst["c_cmpmask"][:, c, :]), writes=[("cmpm", c)])
                P.dma("sp", lambda e: e.dma_start(out=cmask[:], in_=cst["c_cmask"]), writes=["cmask"])
                P.dma("sp", lambda e: e.dma_start(out=amask[:], in_=cst["c_amask"]), writes=["amask"])
                P.dma("sp", lambda e: e.dma_start(out=abias[:], in_=cst["c_abias"]), writes=["abias"])
            pre["loads"] = loads
            return pre

        def phase_D(st, kslc, kwin, kcmp, vslc, vwin, vcmp, pre):
            wq, wo, qT, triA, triB = pre["wq"], pre["wo"], pre["qT"], pre["triA"], pre["triB"]
            cmpm, cmask, amask, abias = pre["cmpm"], pre["cmask"], pre["amask"], pre["abias"]
            xb = sbuf(st, "xbD", [128, KC, TT], BF16)
            gates = sbuf(st, "gates", [128, 4, 48], F32)
            NPB = 4
            pT = [sbuf(st, "pT%d" % i, [128, TT], BF16) for i in range(NPB)]
            ocomb = sbuf(st, "ocomb", [128, 4, D], F32)
            ocb = sbuf(st, "ocb", [128, 4, D], BF16)
            oT = sq
            grp_bufs = []
            for g_ in range(2):
                grp_bufs.append((sbuf(st, "imp%d" % g_, [128, 4, 32], F32), sbuf(st, "impt%d" % g_, [128, 4, 32], F32),
                                 sbuf(st, "imp2_%d" % g_, [128, 4, 32], F32), sbuf(st, "imp3_%d" % g_, [128, 4, 32], F32),
                                 sbuf(st, "m8a%d" % g_, [128, 4, 8], F32), sbuf(st, "m8b%d" % g_, [128, 4, 8], F32),
                                 sbuf(st, "selb%d" % g_, [128, 4, 32], BF16)))
            evt = [sbuf(st, "evt%d" % i, [128, 4, 64], F32) for i in range(2)]
            den = sbuf(st, "den", [128, 3, 4], F32)
            fsc = sbuf(st, "fsc", [128, 3, 4], F32)
            den_c = sbuf(st, "den_c", [128, 2, 4], F32)
            fsc_c = sbuf(st, "fsc_c", [128, 2, 4], F32)
            imptc = sbuf(st, "imptc", [128, 2, 4, 32], F32)

            sbank = [0, 1, 2]
            OC, OS, OW, MISC = 3, 4, 5, 6
            state = {"s": 0, "p": 0}

            def evac(o3, bi, h, okey, first, last, need_max=False):
                hh = h
                if need_max:
                    P.dve(lambda e: e.tensor_scalar(out=den[:, bi, :], in0=o3[:, :, 64], scalar1=1e-30, scalar2=None, op0=ALU.max),
                          reads=[okey], writes=[("den", bi)])
                    P.dve(lambda e: e.reciprocal(out=den[:, bi, :], in_=den[:, bi, :]), reads=[("den", bi)], writes=[("den", bi)])
                else:
                    P.dve(lambda e: e.reciprocal(out=den[:, bi, :], in_=o3[:, :, 64]), reads=[okey], writes=[("den", bi)])
                P.dve(lambda e: e.tensor_tensor(out=fsc[:, bi, :], in0=den[:, bi, :], in1=gates[:, :, bi * 16 + h], op=ALU.mult),
                      reads=[("den", bi), "gates"], writes=[("fsc", bi)])
                fb = fsc[:, bi, :].unsqueeze(2).to_broadcast([128, 4, 64])
                if first:
                    P.dve(lambda e: e.tensor_tensor(out=ocomb[:, :, hh * 64:(hh + 1) * 64], in0=o3[:, :, 0:64], in1=fb, op=ALU.mult),
                          reads=[okey, ("fsc", bi)], writes=[("ocomb", hh)])
                else:
                    tb = evt[bi % 2]
                    P.dve(lambda e: e.tensor_tensor(out=tb[:, :, :], in0=o3[:, :, 0:64], in1=fb, op=ALU.mult),
                          reads=[okey, ("fsc", bi)], writes=[("evt", bi % 2)])
                    if not last:
                        P.dve(lambda e: e.tensor_tensor(out=ocomb[:, :, hh * 64:(hh + 1) * 64], in0=ocomb[:, :, hh * 64:(hh + 1) * 64],
                                                        in1=tb[:, :, :], op=ALU.add),
                              reads=[("evt", bi % 2), ("ocomb", hh)], writes=[("ocomb", hh)])
                    else:
                        P.dve(lambda e: e.tensor_tensor(out=ocb[:, :, h * 64:(h + 1) * 64], in0=ocomb[:, :, hh * 64:(hh + 1) * 64],
                                                        in1=tb[:, :, :], op=ALU.add),
                              reads=[("evt", bi % 2), ("ocomb", hh)], writes=[("ocb", j_, h) for j_ in range(4)])

            norm_tile(0, V_BN, xb, 0, "xbD", stat_ps=MISC, sq_pool=True)
            for c in range(4):
                t0 = c * TT
                for hp in range(8):
                    qb = hp % 3
                    kp = ("ps", qb)
                    for k in range(KC):
                        P.pe(lambda e, k=k, hp=hp, qb=qb: e.matmul(ps[qb][:, :], lhsT=wq[:, k, hp * 128:(hp + 1) * 128], rhs=xb[:, k, :],
                                                                   start=(k == 0), stop=(k == KC - 1)),
                             reads=[("wq", k), ("xbD", k, 0)], writes=[kp])
                    P.dve(lambda e, hp=hp, qb=qb: e.tensor_copy(out=qT[0:64, 2 * hp, :], in_=ps[qb][0:64, :]), reads=[kp], writes=[("qT", 2 * hp)])
                    P.dve(lambda e, hp=hp, qb=qb: e.tensor_copy(out=qT[0:64, 2 * hp + 1, :], in_=ps[qb][64:128, :]),
                          reads=[kp], writes=[("qT", 2 * hp + 1)])
                for j in range(4):
                    for k in range(KC):
                        P.pe(lambda e, k=k, j=j: e.matmul(ps[MISC][:, j * 48:(j + 1) * 48], lhsT=xb[:, k, j * 128:(j + 1) * 128], rhs=wq[:, k, 1024:1072],
                                                          start=(k == 0), stop=(k == KC - 1)),
                             reads=[("wq", k), ("xbD", k, 0)], writes=[("ps", MISC)])
                P.act(lambda e: e.activation(out=gates[:, :, :], in_=ps[MISC][:, 0:192].rearrange("p (j d) -> p j d", d=48), func=AF.Exp, scale=-1.0),
                      reads=[("ps", MISC)], writes=["gates"])
                P.dve(lambda e: e.tensor_scalar(out=gates[:, :, :], in0=gates[:, :, :], scalar1=1.0, scalar2=None, op0=ALU.add),
                      reads=["gates"], writes=["gates"])
                P.dve(lambda e: e.reciprocal(out=gates[:, :, :], in_=gates[:, :, :]), reads=["gates"], writes=["gates"])

                def do_cmp(g, c=c):
                    imp, impt, imp2, imp3, m8a, m8b, selb = grp_bufs[g]
                    nk = 32 * (c + 1) - 1
                    CB = (OC, MISC, OS, OW)
                    def cmp_score(hh, g=g, c=c, nk=nk):
                        h = g * 8 + hh
                        sb_ = sbank[state["s"] % 3]
                        state["s"] += 1
                        sT = ps[sb_]
                        P.pe(lambda e: e.matmul(sT[0:nk, :], lhsT=kcmp[g][0:71, 0:nk], rhs=qT[0:71, h, :], start=True, stop=False),
                             reads=[("kcmp", g), ("kcmp_aug", g), ("qT", h), "qT_aug"], writes=[("ps", sb_)])
                        P.pe(lambda e: e.matmul(sT[0:nk, :], lhsT=ident[0:nk, 0:nk], rhs=cmpm[0:nk, c, :], start=False, stop=True),
                             reads=["ident", ("cmpm", c)], writes=[("ps", sb_)])
                        return sb_

                    def cmp_rest(hh, sb_, g=g, c=c, nk=nk):
                        h = g * 8 + hh
                        pb_i = state["p"] % NPB
                        state["p"] += 1
                        sT = ps[sb_]
                        eT = pT[pb_i]
                        cbk = CB[hh % 4]
                        ocmp = ps[cbk][:, 0:388].rearrange("p (j d) -> p j d", d=97)
                        P.act(lambda e: e.activation(out=eT[0:nk, :], in_=sT[0:nk, :], func=AF.Exp, scale=0.125,
                                                     bias=abias[0:nk, h * 4 + c:h * 4 + c + 1]),
                              reads=[("ps", sb_), "abias"], writes=[("pT", pb_i)])
                        return pb_i, eT, cbk, ocmp

                    def cmp_pv(hh, pb_i, eT, cbk, ocmp, g=g, nk=nk):
                        h = g * 8 + hh
                        for j in range(4):
                            P.pe(lambda e, j=j: e.matmul(ocmp[:, j, :], lhsT=eT[0:nk, j * 128:(j + 1) * 128], rhs=vcmp[g][0:nk, :],
                                                         start=True, stop=True),
                                 reads=[("pT", pb_i), ("vcmp", g), ("vcmp_c", g), ("vcmp_1", g)], writes=[("ps", cbk)])
                        return (ocmp, h, cbk, hh)

                    def cmp_evac_pair(pair, g=g, c=c):
                        stages = []
                        for s_, (ocmp, h, cbk, hh) in enumerate(pair):
                            st_ = []
                            okey = ("ps", cbk)
                            dk, fk, ik = ("den_c", s_), ("fsc_c", s_), ("imptc", s_)
                            if c == 0:
                                st_.append(lambda ocmp=ocmp, s_=s_, okey=okey, dk=dk: P.dve(
                                    lambda e: e.tensor_scalar(out=den_c[:, s_, :], in0=ocmp[:, :, 64], scalar1=1e-30, scalar2=None, op0=ALU.max),
                                    reads=[okey], writes=[dk]))
                                st_.append(lambda s_=s_, dk=dk: P.dve(
                                    lambda e: e.reciprocal(out=den_c[:, s_, :], in_=den_c[:, s_, :]), reads=[dk], writes=[dk]))
                            else:
                                st_.append(lambda ocmp=ocmp, s_=s_, okey=okey, dk=dk: P.dve(
                                    lambda e: e.reciprocal(out=den_c[:, s_, :], in_=ocmp[:, :, 64]), reads=[okey], writes=[dk]))
                            st_.append(lambda s_=s_, h=h, dk=dk, fk=fk: P.dve(
                                lambda e: e.tensor_tensor(out=fsc_c[:, s_, :], in0=den_c[:, s_, :], in1=gates[:, :, h], op=ALU.mult),
                                reads=[dk, "gates"], writes=[fk]))
                            st_.append(lambda ocmp=ocmp, s_=s_, h=h, okey=okey, fk=fk: P.dve(
                                lambda e: e.tensor_tensor(out=ocomb[:, :, h * 64:(h + 1) * 64], in0=ocmp[:, :, 0:64],
                                                          in1=fsc_c[:, s_, :].unsqueeze(2).to_broadcast([128, 4, 64]), op=ALU.mult),
                                reads=[okey, fk], writes=[("ocomb", h)]))
                            if hh == 0:
                                st_.append(lambda ocmp=ocmp, s_=s_, okey=okey, dk=dk: P.dve(
                                    lambda e: e.tensor_tensor(out=imp[:, :, :], in0=ocmp[:, :, 65:97],
                                                              in1=den_c[:, s_, :].unsqueeze(2).to_broadcast([128, 4, 32]), op=ALU.mult),
                                    reads=[okey, dk], writes=[("imp", g)]))
                            else:
                                st_.append(lambda ocmp=ocmp, s_=s_, okey=okey, dk=dk, ik=ik: P.dve(
                                    lambda e: e.tensor_tensor(out=imptc[:, s_, :, :], in0=ocmp[:, :, 65:97],
                                                              in1=den_c[:, s_, :].unsqueeze(2).to_broadcast([128, 4, 32]), op=ALU.mult),
                                    reads=[okey, dk], writes=[ik]))
                                st_.append(lambda s_=s_, ik=ik: P.dve(
                                    lambda e: e.tensor_tensor(out=imp[:, :, :], in0=imp[:, :, :], in1=imptc[:, s_, :, :], op=ALU.add),
                                    reads=[("imp", g), ik], writes=[("imp", g)]))
                            stages.append(st_)
                        n = max(len(x) for x in stages)
                        for i_ in range(n):
                            for st_ in stages:
                                if i_ < len(st_):
                                    st_[i_]()

                    csb = [None] * 8
                    csb[0] = cmp_score(0)
                    csb[1] = cmp_score(1)
                    pair = []
                    for hh in range(8):
                        r = cmp_rest(hh, csb[hh])
                        if hh + 2 < 8:
                            csb[hh + 2] = cmp_score(hh + 2)
                        pair.append(cmp_pv(hh, *r))
                        if len(pair) == 2:
                            cmp_evac_pair(pair)
                            pair = []
                def topk_dve(g, c=c):
                    imp, impt, imp2, imp3, m8a, m8b, selb = grp_bufs[g]
                    P.dve(lambda e, c=c: e.tensor_tensor(out=imp2[:, :, :], in0=imp[:, :, :], in1=cmask[:, c * 4:(c + 1) * 4, :], op=ALU.mult),
                          reads=[("imp", g), "cmask"], writes=[("imp2", g)])
                    P.dve(lambda e, c=c: e.tensor_tensor(out=imp2[:, :, :], in0=imp2[:, :, :], in1=amask[:, c * 4:(c + 1) * 4, :], op=ALU.add),
                          reads=[("imp2", g), "amask"], writes=[("imp2", g)])
                    for j in range(4):
                        P.dve(lambda e, j=j: e.max(out=m8a[:, j, :], in_=imp2[:, j, :]), reads=[("imp2", g)], writes=[("m8a", g, j)])
                    for j in range(4):
                        P.dve(lambda e, j=j: e.match_replace(out=imp3[:, j, :], in_to_replace=m8a[:, j, :], in_values=imp2[:, j, :], imm_value=-3e9),
                              reads=[("imp2", g), ("m8a", g, j)], writes=[("imp3", g, j)])
                    for j in range(4):
                        P.dve(lambda e, j=j: e.max(out=m8b[:, j, :], in_=imp3[:, j, :]), reads=[("imp3", g, j)], writes=[("m8b", g, j)])
                    P.dve(lambda e: e.tensor_tensor(out=selb[:, :, :], in0=imp2[:, :, :], in1=m8b[:, :, 7:8].to_broadcast([128, 4, 32]), op=ALU.is_lt),
                          reads=[("imp2", g)] + [("m8b", g, j) for j in range(4)], writes=[("selb", g)])
                def topk_pe(g, c=c):
                    imp, impt, imp2, imp3, m8a, m8b, selb = grp_bufs[g]
                    for j in range(4):
                        P.pe(lambda e, j=j: e.transpose(pst[0:32, g * TT + j * 128:g * TT + (j + 1) * 128], selb[:, j, :], ident[:]), reads=[("selb", g), "ident"], writes=[("pstk", g)])
                    P.dve(lambda e, g=g: e.tensor_copy(out=qT[96:128, g * 8:(g + 1) * 8, :],
                                                       in_=pst[0:32, g * TT:(g + 1) * TT].unsqueeze(1).to_broadcast([32, 8, TT])),
                          reads=[("pstk", g)], writes=[("qsel", g, j) for j in range(4)])
                def do_work(g, c=c, inject=None):
                    work = []
                    for hh in range(8):
                        h = g * 8 + hh
                        osb, owb = (OS, OW) if hh % 2 == 0 else (OC, MISC)
                        items = []
                        for kt in range(4 * c + 4):
                            i = kt - 4 * c
                            c0 = 0 if i < 0 else 128 * i
                            items.append(("slc", kt, c0, TT, (c0 if i >= 0 else None), "A"))
                        def win_item(i):
                            kt = 4 * c - 4 + i
                            if i < 4:
                                return ("win", kt, 0, 128 * (i + 1), 128 * i, "B")
                            c0 = 128 * (i - 4)
                            return ("win", kt, c0, TT, c0, "A")
                        if c == 0:
                            for i in range(4, 8):
                                items.append(win_item(i))
                        else:
                            for i in range(3):
                                items.append(("win2", win_item(i), win_item(i + 5)))
                            items.append(win_item(3))
                            items.append(win_item(4))
                        for idx, it in enumerate(items):
                            work.append(dict(it=it, h=h, osb=osb, owb=owb, first=(idx == 0),
                                             last_slc=(it[0] == "slc" and (idx + 1 == len(items) or items[idx + 1][0] != "slc")),
                                             last=(idx == len(items) - 1)))

                    def oview(bank):
                        return ps[bank][:, 0:260].rearrange("p (j d) -> p j d", d=65)

                    def score(w, g=g, c=c):
                        h = w["h"]
                        if w["it"][0] == "win2":
                            sb_ = sbank[state["s"] % 3]
                            state["s"] += 1
                            sT = ps[sb_]
                            for (_b, kt, c0, c1, mc, mk) in w["it"][1:]:
                                tri = triA if mk == "A" else triB
                                P.pe(lambda e, kt=kt, c0=c0, c1=c1: e.matmul(sT[:, c0:c1], lhsT=kwin[g][0:71, kt * 128:(kt + 1) * 128], rhs=qT[0:71, h, c0:c1],
                                                                             start=True, stop=False),
                                     reads=[("kwin", g, kt // 4), ("kwin_aug", g), ("qT", h), "qT_aug"], writes=[("ps", sb_)])
                                P.pe(lambda e, mc=mc, tri=tri: e.matmul(sT[:, mc:mc + 128], lhsT=ident[:, :], rhs=tri[:, :], start=False, stop=True),
                                     reads=["ident", "triA", "triB"], writes=[("ps", sb_)])
                            return sb_
                        br, kt, c0, c1, mc, mk = w["it"]
                        if w["first"]:
                            for bank in (w["osb"], w["owb"]):
                                P.pe(lambda e, bank=bank: e.matmul(ps[bank][:, 0:260], lhsT=zer[0:1, 0:128], rhs=zer[0:1, 0:260], start=True, stop=False),
                                     reads=["zer"], writes=[("ps", bank)])
                        sb_ = sbank[state["s"] % 3]
                        state["s"] += 1
                        sT = ps[sb_]
                        if br == "slc":
                            P.pe(lambda e: e.matmul(sT[:, c0:c1], lhsT=kslc[g][:, kt * 128:(kt + 1) * 128], rhs=qT[:, h, c0:c1],
                                                    start=True, stop=(mc is None)),
                                 reads=[("kslc", g, kt // 4), ("kslc_aug", g), ("qT", h), "qT_aug"] + [("qsel", g, jj) for jj in range(c0 // 128, c1 // 128)],
                                 writes=[("ps", sb_)])
                        else:
                            P.pe(lambda e: e.matmul(sT[:, c0:c1], lhsT=kwin[g][0:71, kt * 128:(kt + 1) * 128], rhs=qT[0:71, h, c0:c1],
                                                    start=True, stop=(mc is None)),
                                 reads=[("kwin", g, kt // 4), ("kwin_aug", g), ("qT", h), "qT_aug"], writes=[("ps", sb_)])
                        if mc is not None:
                            tri = triA if mk == "A" else triB
                            P.pe(lambda e: e.matmul(sT[:, mc:mc + 128], lhsT=ident[:, :], rhs=tri[:, :], start=False, stop=True),
                                 reads=["ident", "triA", "triB"], writes=[("ps", sb_)])
                        return sb_

                    def expo(w, sb_, c=c):
                        if w["it"][0] == "win2":
                            br, kt, c0, c1, mc, mk = ("win", None, 0, TT, None, None)
                        else:
                            br, kt, c0, c1, mc, mk = w["it"]
                        h = w["h"]
                        pb_i = state["p"] % NPB
                        state["p"] += 1
                        P.act(lambda e: e.activation(out=pT[pb_i][:, c0:c1], in_=ps[sb_][:, c0:c1], func=AF.Exp, scale=0.125,
                                                     bias=abias[:, h * 4 + c:h * 4 + c + 1]),
                              reads=[("ps", sb_), "abias"], writes=[("pT", pb_i)])
                        return pb_i

                    def pv(w, pb_i, g=g):
                        if w["it"][0] == "win2":
                            ob = w["owb"]
                            o3 = oview(ob)
                            for (_b, kt, c0, c1, mc, mk) in w["it"][1:]:
                                for j in range(c0 // 128, c1 // 128):
                                    P.pe(lambda e, j=j, kt=kt: e.matmul(o3[:, j, :], lhsT=pT[pb_i][:, j * 128:(j + 1) * 128], rhs=vwin[:, g, kt, :],
                                                                        start=False, stop=False),
                                         reads=[("pT", pb_i), ("vwin", kt), "vwin_1"], writes=[("ps", ob)])
                            return
                        br, kt, c0, c1, mc, mk = w["it"]
                        ob = w["osb"] if br == "slc" else w["owb"]
                        o3 = oview(ob)
                        vv = vslc if br == "slc" else vwin
                        fin = w["last_slc"] if br == "slc" else w["last"]
                        jl = c1 // 128 - 1
                        for j in range(c0 // 128, c1 // 128):
                            P.pe(lambda e, j=j: e.matmul(o3[:, j, :], lhsT=pT[pb_i][:, j * 128:(j + 1) * 128], rhs=vv[:, g, kt, :],
                                                         start=False, stop=(fin and j == jl)),
                                 reads=[("pT", pb_i), (("vslc" if br == "slc" else "vwin"), kt), ("vslc_1" if br == "slc" else "vwin_1")],
                                 writes=[("ps", ob)])

                    nw = len(work)
                    sbs = [None] * nw
                    for i0 in range(min(2, nw)):
                        sbs[i0] = score(work[i0])
                    for ii in range(nw):
                        w = work[ii]
                        pb_i = expo(w, sbs[ii])
                        if ii + 2 < nw:
                            sbs[ii + 2] = score(work[ii + 2])
                        pv(w, pb_i)
                        if inject is not None and ii == 5:
                            inject()
                        if w["last_slc"]:
                            evac(oview(w["osb"]), 1, w["h"], ("ps", w["osb"]), False, False)
                        if w["last"]:
                            evac(oview(w["owb"]), 2, w["h"], ("ps", w["owb"]), False, True)
                do_cmp(0)
                do_cmp(1)
                topk_dve(0)
                topk_pe(0)
                topk_dve(1)
                do_work(0, inject=lambda: topk_pe(1))
                do_work(1)
                xb_later = None
                if c + 1 < 4:
                    xb_later = norm_tile(t0 + TT, V_BN, xb, 0, "xbD", stat_ps=MISC, sq_pool=True, defer_xb=True)
                for j in range(4):
                    for m in range(KC):
                        P.pe(lambda e, j=j, m=m: e.transpose(pst[:, m * 128:(m + 1) * 128], ocb[:, j, m * 128:(m + 1) * 128], ident[:]),
                             reads=[("ocb", j, 2 * m), ("ocb", j, 2 * m + 1), "ident"], writes=[("pstk", 0), ("pstk", 1)])
                    P.dve(lambda e, j=j: e.tensor_copy(out=oT[:, :, j * 128:(j + 1) * 128], in_=pst[:, :].rearrange("p (m q) -> p m q", q=128)),
                          reads=[("pstk", 0), ("pstk", 1)], writes=[("sq", k_) for k_ in range(KC)])
                if xb_later is not None:
                    xb_later()
                for mo in range(KC):
                    ob_ = (MISC, OC, OS, OW)[mo % 4]
                    for m in range(KC):
                        P.pe(lambda e, m=m, mo=mo, ob_=ob_: e.matmul(ps[ob_][:, :], lhsT=wo[:, m, mo * 128:(mo + 1) * 128], rhs=oT[:, m, :],
                                                                     start=(m == 0), stop=(m == KC - 1)),
                             reads=[("woD", m), ("sq", m)], writes=[("ps", ob_)])
                    P.dve(lambda e, mo=mo, t0=t0, ob_=ob_: e.tensor_tensor(out=hT[:, mo, t0:t0 + TT], in0=hT[:, mo, t0:t0 + TT], in1=ps[ob_][:, :], op=ALU.add),
                          reads=[("ps", ob_)] + hkeys(mo, t0, TT), writes=hkeys(mo, t0, TT))

        def phase_out(do_norm):
            with contextlib.ExitStack() as st:
                ob = [sbuf(st, "ob%d" % i, [128, KC, TT], F32) for i in range(2)]
                ov = outT.rearrange("(k p) t -> p k t", p=128)
                for t in range(NT):
                    t0 = t * TT
                    o = ob[t % 2]
                    if do_norm:
                        pst_ = ps[6]
                        for k in range(KC):
                            P.act(lambda e, k=k, t0=t0: e.activation(out=sq[:, k, :], in_=hT[:, k, t0:t0 + TT], func=AF.Square),
                                  reads=hkeys(k, t0, TT), writes=[("sq", k)])
                        for k in range(KC):
                            P.pe(lambda e, k=k: e.matmul(ps[6][:, :], lhsT=ones[:], rhs=sq[:, k, :], start=(k == 0), stop=(k == KC - 1)),
                                 reads=["ones", ("sq", k)], writes=[("ps", 6)])
                        P.act(lambda e: e.activation(out=rstd[:], in_=ps[6][:, :], func=AF.Sqrt, scale=1.0 / D, bias=epsc[:]),
                              reads=[("ps", 6), "epsc"], writes=["rstd"])
                        P.dve(lambda e: e.reciprocal(out=rstd[:], in_=rstd[:]), reads=["rstd"], writes=["rstd"])
                        for k in range(KC):
                            P.dve(lambda e, k=k, o=o, t0=t0: e.scalar_tensor_tensor(out=o[:, k, :], in0=hT[:, k, t0:t0 + TT],
                                                                             scalar=vecs[:, V_FIN + k:V_FIN + k + 1], in1=rstd[:],
                                                                             op0=ALU.mult, op1=ALU.mult),
                                  reads=hkeys(k, t0, TT) + ["rstd", "vecs"], writes=[("ob", t % 2, k)])
                    else:
                        for k in range(KC):
                            P.dve(lambda e, k=k, o=o, t0=t0: e.tensor_copy(out=o[:, k, :], in_=hT[:, k, t0:t0 + TT]),
                                  reads=hkeys(k, t0, TT), writes=[("ob", t % 2, k)])
                    for k in range(KC):
                        final_ops.append(P.dma("sp", lambda e, k=k, o=o, t0=t0: e.dma_start(out=ov[:, k, t0:t0 + TT], in_=o[:, k, :]),
                                               reads=[("ob", t % 2, k)]))

        stages = ["A", "B", "C", "D", "E", "F"]
        last = stages.index(stop_after) if stop_after else len(stages) - 1
        phase_A()
        if last >= 1:
            phase_ffn(0)
        if last >= 2:
            phase_CD(do_D=(last >= 3))
        if last >= 4:
            phase_ffn(1, final=(last >= 5))
        if last < 5:
            phase_out(do_norm=False)
        P.emit(final_wait_ops=final_ops)
    return nc, P.stats


_CACHE = {}


def make_in_maps(inputs, consts=None):
    inp = {k: np.asarray(v) for k, v in inputs.items()}
    consts = consts or make_consts()
    vecs = pack_vecs(inp)
    shared = {
        "vecs": vecs,
        "a_pw1_w": np.ascontiguousarray(inp["a_pw1_w"][0], np.float32),
        "a_pw2_w": np.ascontiguousarray(inp["a_pw2_w"][0], np.float32),
        "w_kv": np.ascontiguousarray(inp["w_kv"], np.float32),
        "posTk": np.ascontiguousarray(inp["cmp_pos_k"].T, np.float32),
        "posTv": np.ascontiguousarray(inp["cmp_pos_v"].T, np.float32),
        "phi_k_w1": np.ascontiguousarray(inp["phi_k_w1"], np.float32),
        "phi_k_w2": np.ascontiguousarray(inp["phi_k_w2"], np.float32),
        "phi_v_w1": np.ascontiguousarray(inp["phi_v_w1"], np.float32),
        "phi_v_w2": np.ascontiguousarray(inp["phi_v_w2"], np.float32),
        "b_w_in": np.ascontiguousarray(inp["b_w_in"][0], np.float32),
        "b_w_out": np.ascontiguousarray(inp["b_w_out"][0], np.float32),
        "ffn_w_in0": np.ascontiguousarray(inp["ffn_w_in"][0], np.float32),
        "ffn_w_in1": np.ascontiguousarray(inp["ffn_w_in"][1], np.float32),
        "ffn_w_out0": np.ascontiguousarray(inp["ffn_w_out"][0], np.float32),
        "ffn_w_out1": np.ascontiguousarray(inp["ffn_w_out"][1], np.float32),
    }
    shared.update(consts)
    maps = []
    for b in range(inp["x"].shape[0]):
        m = dict(shared)
        m["xT"] = np.ascontiguousarray(inp["x"][b].T, np.float32)
        maps.append(m)
    return maps


def kernel(**inputs):
    if "nc" not in _CACHE:
        _CACHE["nc"], _CACHE["stats"] = build()
    nc = _CACHE["nc"]
    maps = make_in_maps(inputs)
    n = len(maps)
    res = run_bass_kernel_spmd(nc, maps, core_ids=list(range(n)))
    out = np.stack([np.asarray(r["outT"], np.float32).T for r in res.results], axis=0)
    return np.ascontiguousarray(out, dtype=np.float32)
```

```python
import contextlib
import numpy as np
import ml_dtypes
import concourse.bass as bass
import concourse.mybir as mybir
from concourse.bass_utils import run_bass_kernel_spmd

F32 = mybir.dt.float32
BF16 = mybir.dt.bfloat16
ALU = mybir.AluOpType
AF = mybir.ActivationFunctionType

D = 1024
S = 2048
KC = 8
FF = 2816
FC = 22
H = 16
DH = 64
NCMP = 127
BIG = float(2 ** 18)
EPS = 1e-6
FORCE = 1e9
TT = 512
NT = S // TT


class Op:
    __slots__ = ("eng", "fn", "deps", "needed", "sem", "val", "is_dma", "slot", "vc", "gi")


class Prog:
    ENGS = ("pe", "act", "dve", "pool", "sp")

    def __init__(self, nc, n_dma_sems=32):
        self.nc = nc
        self.ops = []
        self.last_w = {}
        self.readers = {}
        self.n_dma_sems = n_dma_sems
        self.dma_rr = 0
        self.dma_rr_q = {}
        self.dma_last = {}
        self.last_on = {}

    def add(self, eng, fn, reads=(), writes=(), dma=False):
        op = Op()
        op.eng = eng
        op.fn = fn
        op.needed = False
        op.is_dma = dma
        op.sem = None
        op.val = None
        op.slot = None
        op.gi = len(self.ops)
        deps = []
        for r in reads:
            w = self.last_w.get(r)
            if w is not None:
                deps.append(w)
        for w_ in writes:
            w = self.last_w.get(w_)
            if w is not None:
                deps.append(w)
            deps.extend(self.readers.get(w_, ()))
        if dma:
            nper = self.n_dma_sems // 2
            qi = 1 if eng == "pool" else 0
            rr = self.dma_rr_q.get(qi, 0)
            self.dma_rr_q[qi] = rr + 1
            slot = qi * nper + rr % nper
            op.slot = slot
            prev = self.dma_last.get(slot)
            if prev is not None:
                deps.append(prev)
            self.dma_last[slot] = op
        seen = set()
        fdeps = []
        for d in deps:
            if d is op or id(d) in seen:
                continue
            seen.add(id(d))
            if (not d.is_dma) and (not dma) and d.eng == eng and eng == "pe":
                continue
            fdeps.append(d)
        op.deps = fdeps
        for w_ in writes:
            self.last_w[w_] = op
            self.readers[w_] = []
        for r in reads:
            if r in writes:
                continue
            self.readers.setdefault(r, []).append(op)
        self.ops.append(op)
        if not dma:
            self.last_on[eng] = op
        return op

    def pe(self, fn, reads=(), writes=()):
        return self.add("pe", fn, reads, writes)

    def act(self, fn, reads=(), writes=()):
        return self.add("act", fn, reads, writes)

    def dve(self, fn, reads=(), writes=()):
        return self.add("dve", fn, reads, writes)

    def pool(self, fn, reads=(), writes=()):
        return self.add("pool", fn, reads, writes)

    def dma(self, q, fn, reads=(), writes=()):
        return self.add(q, fn, reads, writes, dma=True)

    def barrier(self):
        deps = [o for o in self.last_on.values()] + [o for o in self.dma_last.values()]
        for e in self.ENGS:
            op = Op()
            op.eng = e
            op.fn = None
            op.needed = False
            op.is_dma = False
            op.sem = None
            op.val = None
            op.slot = None
            op.gi = len(self.ops)
            op.deps = list(deps)
            self.ops.append(op)
        self.last_w = {}
        self.readers = {}

    def emit(self, final_wait_ops=()):
        nc = self.nc
        engs = {"pe": nc.tensor, "act": nc.scalar, "dve": nc.vector, "pool": nc.gpsimd, "sp": nc.sync}
        for op in self.ops:
            for d in op.deps:
                d.needed = True
        for op in final_wait_ops:
            op.needed = True
        with contextlib.ExitStack() as st:
            esem = {e: st.enter_context(nc.semaphore("s_" + e)) for e in engs}
            dsem = [st.enter_context(nc.semaphore("d_%d" % i)) for i in range(self.n_dma_sems)]
            ecount = {e: 0 for e in engs}
            dcount = [0] * self.n_dma_sems
            seen = {e: {} for e in engs}
            nwait = 0
            nfused = 0
            for op in self.ops:
                e = engs[op.eng]
                sn = seen[op.eng]
                pend = []
                for d in sorted(op.deps, key=lambda o: -o.gi):
                    if d.fn is None:
                        continue
                    assert d.val is not None
                    key = d.sem.name
                    if sn.get(key, 0) < d.val:
                        pend.append(d)
                        nwait += 1
                        sn[key] = d.val
                    for k_, v_ in d.vc.items():
                        if sn.get(k_, 0) < v_:
                            sn[k_] = v_
                fuse = op.fn is not None and not op.is_dma and len(pend) > 0
                for d in (pend[:-1] if fuse else pend):
                    e.wait_ge(d.sem, d.val)
                if op.fn is None:
                    continue
                op.vc = dict(sn)
                ins = op.fn(e)
                if fuse:
                    ins._wait_ge(pend[-1].sem, pend[-1].val)
                    nfused += 1
                if op.is_dma:
                    dcount[op.slot] += 16
                    op.sem = dsem[op.slot]
                    op.val = dcount[op.slot]
                    ins.then_inc(op.sem, 16)
                elif op.needed:
                    ecount[op.eng] += 1
                    op.sem = esem[op.eng]
                    op.val = ecount[op.eng]
                    ins.then_inc(op.sem, 1)
            for op in final_wait_ops:
                nc.sync.wait_ge(op.sem, op.val)
            self.stats = dict(n_ops=len(self.ops), n_wait=nwait, n_fused=nfused, ecount=dict(ecount))


def _bf(x):
    return np.asarray(x, np.float32).astype(ml_dtypes.bfloat16).astype(np.float32)


def make_consts():
    c = {}
    c["c_ident"] = np.eye(128, dtype=np.float32)
    kl = np.arange(128)[:, None]
    ql = np.arange(128)[None, :]
    c["c_triA"] = np.where(kl > ql, -BIG, 0.0).astype(np.float32)
    c["c_triB"] = np.where(ql >= kl, -BIG, 0.0).astype(np.float32)
    n = np.arange(128)[:, None, None]
    t = (np.arange(4)[None, :, None] * TT + np.arange(TT)[None, None, :])
    c["c_cmpmask"] = np.where(t < 16 * n + 31, -BIG, 0.0).astype(np.float32)
    key = np.arange(S)[None, :]
    j = np.arange(32)[:, None]
    c["c_E"] = np.where(key // 64 == j, -BIG, 0.0).astype(np.float32)
    slopes = np.asarray(2.0 ** (-8.0 * (np.arange(H) + 1) / H), np.float32)
    qaug = np.zeros((7, H, TT), np.float32)
    for h in range(H):
        s = np.float32(slopes[h])
        s1 = _bf(s)
        s2 = _bf(np.float32(s - s1))
        s3 = _bf(np.float32(s - s1 - s2))
        qaug[0, h, :] = _bf(-8.0 * s * np.arange(TT, dtype=np.float32))
        for r, sv in enumerate((s1, s2, s3)):
            qaug[1 + r, h, :] = 8.0 * sv
            qaug[4 + r, h, :] = 8.0 * sv
    c["c_qaug"] = qaug
    kaug = np.zeros((7, S), np.float32)
    kk = np.arange(S)
    kaug[0] = 1.0
    kaug[1:4] = (kk % 128)[None, :]
    kaug[4:7] = (kk - kk % 128)[None, :]
    c["c_kaug"] = kaug
    kc = np.zeros((7, 128), np.float32)
    kc[0] = 1.0
    kc[1:4] = (16 * np.arange(128))[None, :]
    kc[4:7] = 31.0
    c["c_kaugc"] = kc
    p = np.arange(128)[:, None, None]
    sj = np.arange(16)[None, :, None]
    jb = np.arange(32)[None, None, :]
    tq = 128 * sj + p
    tb = tq // 64
    forced = (jb == 0) | (jb == tb) | (jb == tb - 1)
    causal = jb * 64 <= tq
    c["c_cmask"] = (causal & ~forced).astype(np.float32)
    c["c_amask"] = np.where(forced, FORCE, np.where(causal, 0.0, -FORCE)).astype(np.float32)
    ab = np.zeros((128, H * 4), np.float32)
    for h in range(H):
        for cc in range(4):
            ab[:, h * 4 + cc] = -np.float32(slopes[h]) * np.float32(TT * cc)
    c["c_abias"] = ab
    cs = np.arange(128)[:, None] * 16
    ss = np.arange(32)[None, :] * 64
    ov = ((cs <= ss + 63) & (cs + 31 >= ss)).astype(np.float32)
    ov[127:] = 0.0
    c["c_ovl"] = ov
    return c


CONST_SHAPES = {
    "c_ident": [128, 128], "c_triA": [128, 128], "c_triB": [128, 128], "c_cmpmask": [128, 4, TT],
    "c_E": [32, S], "c_qaug": [7, H, TT], "c_kaug": [7, S], "c_kaugc": [7, 128],
    "c_cmask": [128, 16, 32], "c_amask": [128, 16, 32], "c_abias": [128, H * 4], "c_ovl": [128, 32],
}

V_ANORM = 0
V_PW1B = 8
V_DWW = 24
V_DWB = 24 + 248
V_LNG = V_DWB + 8
V_LNB = V_LNG + 8
V_PW2B = V_LNB + 8
V_FFN0 = V_PW2B + 8
V_KVN = V_FFN0 + 8
V_BN = V_KVN + 8
V_FFN1 = V_BN + 8
V_FIN = V_FFN1 + 8
NV = V_FIN + 8


def pack_vecs(inp):
    def pk(v):
        v = np.asarray(v, np.float32).reshape(-1, 128)
        return v.T
    cols = [pk(inp["a_norm"][0]), pk(inp["a_pw1_b"][0]),
            np.asarray(inp["a_dw_w"][0], np.float32).reshape(31, 8, 128).transpose(2, 0, 1).reshape(128, 248),
            pk(inp["a_dw_b"][0]), pk(inp["a_ln_g"][0]), pk(inp["a_ln_b"][0]), pk(inp["a_pw2_b"][0]),
            pk(inp["ffn_norm"][0]), pk(inp["kv_norm"]), pk(inp["b_norm"][0]), pk(inp["ffn_norm"][1]),
            pk(inp["final_norm"])]
    out = np.ascontiguousarray(np.concatenate(cols, axis=1), dtype=np.float32)
    assert out.shape == (128, NV)
    return out


def build(stop_after=None, dbg_kv=False):
    nc = bass.Bass("TRN2", target_bir_lowering=False)

    def din(name, shape):
        return nc.dram_tensor(name, list(shape), F32, kind="ExternalInput").ap()

    xT = din("xT", [D, S])
    vecs_d = din("vecs", [128, NV])
    w_pw1 = din("a_pw1_w", [D, 2 * D])
    w_pw2 = din("a_pw2_w", [D, D])
    w_kv = din("w_kv", [D, 768])
    posk = din("posTk", [64, 32])
    posv = din("posTv", [64, 32])
    w1k = din("phi_k_w1", [2048, 64])
    w2k = din("phi_k_w2", [64, 64])
    w1v = din("phi_v_w1", [2048, 64])
    w2v = din("phi_v_w2", [64, 64])
    w_bin = din("b_w_in", [D, 1072])
    w_bout = din("b_w_out", [D, D])
    w_fin = [din("ffn_w_in0", [D, 2 * FF]), din("ffn_w_in1", [D, 2 * FF])]
    w_fout = [din("ffn_w_out0", [FF, D]), din("ffn_w_out1", [FF, D])]
    cst = {k: din(k, v) for k, v in CONST_SHAPES.items()}
    outT = nc.dram_tensor("outT", [D, S], F32, kind="ExternalOutput").ap()
    dbg = {}
    if dbg_kv:
        for nm, shp in (("d_kslc", [2, 128, S]), ("d_kwin", [2, 128, S]), ("d_kcmp", [2, 128, 128]),
                        ("d_vslc", [128, 2 * 16 * 65]), ("d_vwin", [128, 2 * 16 * 65]), ("d_vcmp", [2, 128, 97])):
            dbg[nm] = nc.dram_tensor(nm, shp, F32, kind="ExternalOutput").ap()

    P = Prog(nc)
    final_ops = []
    with contextlib.ExitStack() as g_st:
        name_ctr = {}

        def sbuf(st, name, shape, dt):
            n = name_ctr.get(name, 0)
            name_ctr[name] = n + 1
            if n:
                name = "%s_r%d" % (name, n)
            return st.enter_context(nc.sbuf_tensor(name, list(shape), dt))

        hT = sbuf(g_st, "hT", [128, KC, S], F32)
        vecs = sbuf(g_st, "vecs_sb", [128, NV], F32)
        ident = sbuf(g_st, "ident", [128, 128], BF16)
        ones = sbuf(g_st, "ones", [128, 128], BF16)
        epsc = sbuf(g_st, "epsc", [128, 1], F32)
        zer = sbuf(g_st, "zer", [128, 260], BF16)
        rstd = sbuf(g_st, "rstd", [128, TT], F32)
        sq = sbuf(g_st, "sq", [128, KC, TT], BF16)
        ps = [g_st.enter_context(nc.psum_tensor("ps%d" % i, [128, 512], F32)) for i in range(7)]
        pst = g_st.enter_context(nc.psum_tensor("pst", [128, 1024], BF16))

        P.dve(lambda e: e.memset(ones[:], 1.0), writes=["ones"])
        P.dve(lambda e: e.memset(epsc[:], EPS), writes=["epsc"])
        P.dve(lambda e: e.memset(zer[:], 0.0), writes=["zer"])
        P.dma("sp", lambda e: e.dma_start(out=vecs[:], in_=vecs_d), writes=["vecs"])
        P.dma("pool", lambda e: e.dma_start(out=ident[:], in_=cst["c_ident"]), writes=["ident"])
        xT_v = xT.rearrange("(k p) t -> p k t", p=128)
        for t in range(NT):
            for k in range(KC):
                P.dma("sp", lambda e, k=k, t=t: e.dma_start(out=hT[:, k, t * TT:(t + 1) * TT], in_=xT_v[:, k, t * TT:(t + 1) * TT]),
                      writes=[("h", k, t)])

        def hkeys(k, t0, T):
            return [("h", k, t) for t in range(t0 // TT, (t0 + T - 1) // TT + 1)]

        def norm_tile(t0, gcol, xb, xoff, xkey, stat_ps=6, sq_pool=False, defer_xb=False, part=None):
            pst_ = ps[stat_ps]
            for k in (range(KC) if part in (None, "sq") else ()):
                if sq_pool and k % 2 == 1:
                    P.pool(lambda e, k=k: e.tensor_tensor(out=sq[:, k, :], in0=hT[:, k, t0:t0 + TT], in1=hT[:, k, t0:t0 + TT], op=ALU.mult),
                           reads=hkeys(k, t0, TT), writes=[("sq", k)])
                else:
                    P.act(lambda e, k=k: e.activation(out=sq[:, k, :], in_=hT[:, k, t0:t0 + TT], func=AF.Square),
                          reads=hkeys(k, t0, TT), writes=[("sq", k)])
            if part == "sq":
                return None
            for k in range(KC):
                P.pe(lambda e, k=k: e.matmul(pst_[:, :], lhsT=ones[:], rhs=sq[:, k, :], start=(k == 0), stop=(k == KC - 1)),
                     reads=["ones", ("sq", k)], writes=[("ps", stat_ps)])
            P.act(lambda e: e.activation(out=rstd[:], in_=pst_[:, :], func=AF.Ln, scale=1.0 / D, bias=epsc[:]),
                  reads=[("ps", stat_ps), "epsc"], writes=["rstd"])
            P.act(lambda e: e.activation(out=rstd[:], in_=rstd[:], func=AF.Exp, scale=-0.5), reads=["rstd"], writes=["rstd"])
            def xb_part():
                for k in range(KC):
                    P.dve(lambda e, k=k: e.scalar_tensor_tensor(out=xb[:, k, xoff:xoff + TT], in0=hT[:, k, t0:t0 + TT],
                                                                scalar=vecs[:, gcol + k:gcol + k + 1], in1=rstd[:],
                                                                op0=ALU.mult, op1=ALU.mult),
                          reads=hkeys(k, t0, TT) + ["rstd", "vecs"], writes=[(xkey, k, xoff // TT)])
            if defer_xb:
                return xb_part
            xb_part()
            return None

        def load_w(dst, src, rows_k, key, q="pool", col0=0, ncols=None):
            sv = src.rearrange("(k p) n -> p k n", p=128)
            if ncols is None:
                ncols = src.shape[1]
            step = 4 if ncols <= 512 else 1
            for k0 in range(0, rows_k, step):
                k1 = min(rows_k, k0 + step)
                P.dma(q, lambda e, k0=k0, k1=k1: e.dma_start(out=dst[:, k0:k1, 0:ncols], in_=sv[:, k0:k1, col0:col0 + ncols]),
                      writes=[(key, k) for k in range(k0, k1)])

        def phase_A():
            with contextlib.ExitStack() as st:
                hglu = sbuf(st, "hglu", [128, KC, 30 + S], BF16)
                diagA = sbuf(st, "diagA", [128, 31 * 4, 128], BF16)
                dstate = {"B": None, "n": 0}

                def dg(j, m):
                    t_ = diagA if m < 4 else dstate["B"]
                    return t_[:, j * 4 + (m % 4), :]

                def build_diag_one(j, m):
                    jm = j * 8 + m
                    dstate["n"] += 1
                    if dstate["n"] % 2 == 0:
                        P.dve(lambda e: e.tensor_scalar(out=dg(j, m), in0=ident[:, :], scalar1=vecs[:, V_DWW + jm:V_DWW + jm + 1],
                                                        scalar2=None, op0=ALU.mult),
                              reads=["ident", "vecs"], writes=[("diag", jm)])
                    else:
                        P.act(lambda e: e.activation(out=dg(j, m), in_=ident[:, :], func=AF.Copy,
                                                     scale=vecs[:, V_DWW + jm:V_DWW + jm + 1]),
                              reads=["ident", "vecs"], writes=[("diag", jm)])
                early_list = [(j, m) for m in range(4) for j in range(31)]
                with contextlib.ExitStack() as st1:
                    wA1 = sbuf(st1, "wA1", [128, KC, 2 * D], BF16)
                    xbs = [sbuf(st1, "xbA%d" % i, [128, KC, TT], BF16) for i in range(2)]
                    sig = [sbuf(st1, "sig%d" % i, [128, TT], F32) for i in range(2)]
                    sv1 = w_pw1.rearrange("(k p) n -> p k n", p=128)
                    for cb in (0, 2, 1, 3):
                        for k0 in (0, 4):
                            P.dma("pool", lambda e, cb=cb, k0=k0: e.dma_start(out=wA1[:, k0:k0 + 4, cb * 512:(cb + 1) * 512],
                                                                              in_=sv1[:, k0:k0 + 4, cb * 512:(cb + 1) * 512]),
                                  writes=[("wA1", cb, k0)])
                    P.dve(lambda e: e.memset(hglu[:, :, 0:30], 0.0), writes=[("hg", k, -1) for k in range(KC)])
                    norm_tile(0, V_ANORM, xbs[0], 0, ("xb", 0))
                    for t in range(NT):
                        t0 = t * TT
                        xb = xbs[t % 2]
                        xk = ("xb", t % 2)
                        if t + 1 < NT:
                            norm_tile(t0 + TT, V_ANORM, xbs[(t + 1) % 2], 0, ("xb", (t + 1) % 2))
                        for m in range(KC):
                            pa, pb = ps[m % 2], ps[2 + m % 2]
                            for k in range(KC):
                                P.pe(lambda e, k=k, m=m, pa=pa, xb=xb: e.matmul(pa[:, :], lhsT=wA1[:, k, m * 128:(m + 1) * 128], rhs=xb[:, k, :],
                                                                                 start=(k == 0), stop=(k == KC - 1)),
                                     reads=[("wA1", m // 4, (k // 4) * 4), (xk, k, 0)], writes=[("ps", m % 2)])
                            for k in range(KC):
                                P.pe(lambda e, k=k, m=m, pb=pb, xb=xb: e.matmul(pb[:, :], lhsT=wA1[:, k, D + m * 128:D + (m + 1) * 128], rhs=xb[:, k, :],
                                                                                 start=(k == 0), stop=(k == KC - 1)),
                                     reads=[("wA1", 2 + m // 4, (k // 4) * 4), (xk, k, 0)], writes=[("ps", 2 + m % 2)])
                            sg = sig[m % 2]
                            P.act(lambda e, m=m, pb=pb, sg=sg: e.activation(out=sg[:], in_=pb[:, :], func=AF.Sigmoid,
                                                                             bias=vecs[:, V_PW1B + 8 + m:V_PW1B + 9 + m]),
                                  reads=[("ps", 2 + m % 2), "vecs"], writes=[("sig", m % 2)])
                            P.dve(lambda e, m=m, pa=pa, sg=sg, t0=t0: e.scalar_tensor_tensor(out=hglu[:, m, 30 + t0:30 + t0 + TT], in0=pa[:, :],
                                                                                             scalar=vecs[:, V_PW1B + m:V_PW1B + m + 1], in1=sg[:],
                                                                                             op0=ALU.add, op1=ALU.mult),
                                  reads=[("ps", m % 2), ("sig", m % 2), "vecs"], writes=[("hg", m, t)])
                            for _ in range(4):
                                if early_list:
                                    build_diag_one(*early_list.pop(0))
                P.barrier()
                with contextlib.ExitStack() as st2:
                    dstate["B"] = sbuf(st2, "diagB", [128, 31 * 4, 128], BF16)
                    wA2 = sbuf(st2, "wA2", [128, KC, D], BF16)
                    cz = sbuf(st2, "cz", [128, KC, TT], BF16)
                    tmp = [sbuf(st2, "lntmp%d" % i, [128, TT], F32) for i in range(2)]
                    mean = sbuf(st2, "mean", [128, TT], F32)
                    msq = sbuf(st2, "msq", [128, TT], F32)
                    rs2 = sbuf(st2, "rs2", [128, TT], F32)
                    load_w(wA2, w_pw2, KC, "wA2")
                    while early_list:
                        build_diag_one(*early_list.pop(0))
                    for m in range(4, KC):
                        for j in range(31):
                            build_diag_one(j, m)
                    cstate = {"cnt": 0}
                    pending = {}

                    def conv_mm(t, m):
                        t0 = t * TT
                        bank = cstate["cnt"] % 4
                        cstate["cnt"] += 1
                        pc = ps[bank]
                        kc_ = ("ps", bank)
                        for j in range(31):
                            lo = t0 + j
                            rk = [("hg", m, tt_) for tt_ in range((lo - 30) // TT if lo >= 30 else -1, (lo + TT - 1 - 30) // TT + 1)]
                            P.pe(lambda e, j=j, lo=lo: e.matmul(pc[:, :], lhsT=dg(j, m), rhs=hglu[:, m, lo:lo + TT],
                                                                start=(j == 0), stop=(j == 30)),
                                 reads=[("diag", j * 8 + m)] + rk, writes=[kc_])
                        pending[(t, m)] = bank

                    def conv_evac(t, m):
                        bank = pending.pop((t, m))
                        pc = ps[bank]
                        kc_ = ("ps", bank)
                        if m % 2 == 0:
                            P.act(lambda e: e.activation(out=cz[:, m, :], in_=pc[:, :], func=AF.Identity,
                                                         bias=vecs[:, V_DWB + m:V_DWB + m + 1]),
                                  reads=[kc_, "vecs"], writes=[("cz", m)])
                        else:
                            P.dve(lambda e: e.tensor_scalar(out=cz[:, m, :], in0=pc[:, :], scalar1=vecs[:, V_DWB + m:V_DWB + m + 1],
                                                            scalar2=None, op0=ALU.add),
                                  reads=[kc_, "vecs"], writes=[("cz", m)])

                    for m in range(KC):
                        conv_mm(0, m)
                        conv_evac(0, m)
                    for t in range(NT):
                        t0 = t * TT
                        for m in range(KC):
                            P.act(lambda e, m=m: e.activation(out=sq[:, m, :], in_=cz[:, m, :], func=AF.Square),
                                  reads=[("cz", m)], writes=[("sq", m)])
                        for m in range(KC):
                            P.pe(lambda e, m=m: e.matmul(ps[4][:, :], lhsT=ones[:], rhs=cz[:, m, :], start=(m == 0), stop=(m == KC - 1)),
                                 reads=["ones", ("cz", m)], writes=[("ps", 4)])
                        for m in range(KC):
                            P.pe(lambda e, m=m: e.matmul(ps[5][:, :], lhsT=ones[:], rhs=sq[:, m, :], start=(m == 0), stop=(m == KC - 1)),
                                 reads=["ones", ("sq", m)], writes=[("ps", 5)])
                        if t + 1 < NT:
                            for m in range(4):
                                conv_mm(t + 1, m)
                        P.dve(lambda e: e.tensor_scalar(out=mean[:], in0=ps[4][:, :], scalar1=1.0 / D, scalar2=None, op0=ALU.mult),
                              reads=[("ps", 4)], writes=["mean"])
                        P.dve(lambda e: e.tensor_tensor(out=msq[:], in0=mean[:], in1=mean[:], op=ALU.mult), reads=["mean"], writes=["msq"])
                        P.dve(lambda e: e.scalar_tensor_tensor(out=msq[:], in0=ps[5][:, :], scalar=1.0 / D, in1=msq[:],
                                                               op0=ALU.mult, op1=ALU.subtract),
                              reads=[("ps", 5), "msq"], writes=["msq"])
                        P.act(lambda e: e.activation(out=rs2[:], in_=msq[:], func=AF.Ln, bias=epsc[:]),
                              reads=["msq", "epsc"], writes=["rs2"])
                        P.act(lambda e: e.activation(out=rs2[:], in_=rs2[:], func=AF.Exp, scale=-0.5), reads=["rs2"], writes=["rs2"])
                        for m in range(KC):
                            tm = tmp[m % 2]
                            P.dve(lambda e, m=m, tm=tm: e.tensor_tensor(out=tm[:], in0=cz[:, m, :], in1=mean[:], op=ALU.subtract),
                                  reads=[("cz", m), "mean"], writes=[("lntmp", m % 2)])
                            P.dve(lambda e, m=m, tm=tm: e.tensor_tensor(out=tm[:], in0=tm[:], in1=rs2[:], op=ALU.mult),
                                  reads=[("lntmp", m % 2), "rs2"], writes=[("lntmp", m % 2)])
                            P.act(lambda e, m=m, tm=tm: e.activation(out=cz[:, m, :], in_=tm[:], func=AF.Silu,
                                                                     scale=vecs[:, V_LNG + m:V_LNG + m + 1], bias=vecs[:, V_LNB + m:V_LNB + m + 1]),
                                  reads=[("lntmp", m % 2), "vecs"], writes=[("cz", m)])
                        for mo in range(KC):
                            po = ps[4 + mo % 2]
                            kpo = ("ps", 4 + mo % 2)
                            for m in range(KC):
                                P.pe(lambda e, m=m, mo=mo, po=po: e.matmul(po[:, :], lhsT=wA2[:, m, mo * 128:(mo + 1) * 128], rhs=cz[:, m, :],
                                                                            start=(m == 0), stop=(m == KC - 1)),
                                     reads=[("wA2", m), ("cz", m)], writes=[kpo])
                            P.dve(lambda e, mo=mo, po=po, t0=t0: e.scalar_tensor_tensor(out=hT[:, mo, t0:t0 + TT], in0=po[:, :],
                                                                                        scalar=vecs[:, V_PW2B + mo:V_PW2B + mo + 1],
                                                                                        in1=hT[:, mo, t0:t0 + TT], op0=ALU.add, op1=ALU.add),
                                  reads=[kpo, "vecs"] + hkeys(mo, t0, TT), writes=hkeys(mo, t0, TT))
                        if t + 1 < NT:
                            for m in range(4):
                                conv_evac(t + 1, m)
                            for m in range(4, KC):
                                conv_mm(t + 1, m)
                                conv_evac(t + 1, m)
            P.barrier()

        ob_state = {"i": 0}

        def final_tile(t0, obf):
            ov = outT.rearrange("(k p) t -> p k t", p=128)
            for k in range(KC):
                P.act(lambda e, k=k: e.activation(out=sq[:, k, :], in_=hT[:, k, t0:t0 + TT], func=AF.Square),
                      reads=hkeys(k, t0, TT), writes=[("sq", k)])
            for k in range(KC):
                P.pe(lambda e, k=k: e.matmul(ps[6][:, :], lhsT=ones[:], rhs=sq[:, k, :], start=(k == 0), stop=(k == KC - 1)),
                     reads=["ones", ("sq", k)], writes=[("ps", 6)])
            P.act(lambda e: e.activation(out=rstd[:], in_=ps[6][:, :], func=AF.Ln, scale=1.0 / D, bias=epsc[:]),
                  reads=[("ps", 6), "epsc"], writes=["rstd"])
            P.act(lambda e: e.activation(out=rstd[:], in_=rstd[:], func=AF.Exp, scale=-0.5), reads=["rstd"], writes=["rstd"])
            for k in range(KC):
                i = ob_state["i"] % len(obf)
                ob_state["i"] += 1
                o = obf[i]
                P.dve(lambda e, k=k, o=o: e.scalar_tensor_tensor(out=o[:, :], in0=hT[:, k, t0:t0 + TT],
                                                                 scalar=vecs[:, V_FIN + k:V_FIN + k + 1], in1=rstd[:],
                                                                 op0=ALU.mult, op1=ALU.mult),
                      reads=hkeys(k, t0, TT) + ["rstd", "vecs"], writes=[("obf", i)])
                final_ops.append(P.dma("sp", lambda e, k=k, o=o: e.dma_start(out=ov[:, k, t0:t0 + TT], in_=o[:, :]),
                                       reads=[("obf", i)]))

        def phase_ffn(l, final=False):
            gcol = V_FFN0 if l == 0 else V_FFN1
            win_d, wout_d = w_fin[l], w_fout[l]
            HB = 1024
            with contextlib.ExitStack() as st:
                xb = sbuf(st, "xbF", [128, KC, HB], BF16)
                actT = sbuf(st, "actT", [128, FC, HB], BF16)
                NB = 3
                wg = [sbuf(st, "wg%d" % i, [128, KC, 256], BF16) for i in range(NB)]
                wu = [sbuf(st, "wu%d" % i, [128, KC, 256], BF16) for i in range(NB)]
                wo = [sbuf(st, "wo%d" % i, [128, FC, 256], BF16) for i in range(2)]
                sgt = [sbuf(st, "sgt%d" % i, [128, TT], F32) for i in range(2)]
                obf = [sbuf(st, "obf%d" % i, [128, TT], F32) for i in range(4)] if final else None
                cnt = 0
                for hb in range(S // HB):
                    if hb == 0:
                        for tt in range(HB // TT):
                            norm_tile(hb * HB + tt * TT, gcol, xb, tt * TT, "xbF")
                    for blk in range(FF // 256):
                        b = blk % NB
                        load_w(wg[b], win_d, KC, ("wg", b), col0=blk * 256, ncols=256)
                        load_w(wu[b], win_d, KC, ("wu", b), col0=FF + blk * 256, ncols=256)
                        for fi in range(2):
                            f = blk * 2 + fi
                            for tt in range(HB // TT):
                                pg, pu = ps[cnt % 2], ps[2 + cnt % 2]
                                kg, ku = ("ps", cnt % 2), ("ps", 2 + cnt % 2)
                                for k in range(KC):
                                    P.pe(lambda e, k=k, b=b, fi=fi, tt=tt, pg=pg: e.matmul(pg[:, :], lhsT=wg[b][:, k, fi * 128:(fi + 1) * 128],
                                                                                           rhs=xb[:, k, tt * TT:(tt + 1) * TT],
                                                                                           start=(k == 0), stop=(k == KC - 1)),
                                         reads=[(("wg", b), k), ("xbF", k, tt)], writes=[kg])
                                for k in range(KC):
                                    P.pe(lambda e, k=k, b=b, fi=fi, tt=tt, pu=pu: e.matmul(pu[:, :], lhsT=wu[b][:, k, fi * 128:(fi + 1) * 128],
                                                                                           rhs=xb[:, k, tt * TT:(tt + 1) * TT],
                                                                                           start=(k == 0), stop=(k == KC - 1)),
                                         reads=[(("wu", b), k), ("xbF", k, tt)], writes=[ku])
                                sg = sgt[cnt % 2]
                                P.act(lambda e, pg=pg, sg=sg: e.activation(out=sg[:], in_=pg[:, :], func=AF.Silu),
                                      reads=[kg], writes=[("sgt", cnt % 2)])
                                P.dve(lambda e, f=f, tt=tt, pu=pu, sg=sg: e.tensor_tensor(out=actT[:, f, tt * TT:(tt + 1) * TT], in0=sg[:], in1=pu[:, :],
                                                                                           op=ALU.mult),
                                      reads=[ku, ("sgt", cnt % 2)], writes=[("actT", f, tt)])
                                cnt += 1
                    for ob in range(D // 256):
                        if ob == 1 and hb + 1 < S // HB:
                            for tt in range(HB // TT):
                                norm_tile((hb + 1) * HB + tt * TT, gcol, xb, tt * TT, "xbF")
                        b = ob % 2
                        sv = wout_d.rearrange("(f p) n -> p f n", p=128)
                        for f0 in range(0, FC, 6):
                            f1 = min(FC, f0 + 6)
                            P.dma("pool", lambda e, b=b, ob=ob, f0=f0, f1=f1: e.dma_start(out=wo[b][:, f0:f1, :], in_=sv[:, f0:f1, ob * 256:(ob + 1) * 256]),
                                  writes=[(("wo", b), f) for f in range(f0, f1)])
                        for mi in range(2):
                            m = ob * 2 + mi
                            for tt in range(HB // TT):
                                po = ps[4 + cnt % 2]
                                kp = ("ps", 4 + cnt % 2)
                                t0 = hb * HB + tt * TT
                                for f in range(FC):
                                    P.pe(lambda e, f=f, b=b, mi=mi, tt=tt, po=po: e.matmul(po[:, :], lhsT=wo[b][:, f, mi * 128:(mi + 1) * 128],
                                                                                           rhs=actT[:, f, tt * TT:(tt + 1) * TT],
                                                                                           start=(f == 0), stop=(f == FC - 1)),
                                         reads=[(("wo", b), f), ("actT", f, tt)], writes=[kp])
                                P.dve(lambda e, m=m, t0=t0, po=po: e.tensor_tensor(out=hT[:, m, t0:t0 + TT], in0=hT[:, m, t0:t0 + TT], in1=po[:, :],
                                                                                    op=ALU.add),
                                      reads=[kp] + hkeys(m, t0, TT), writes=hkeys(m, t0, TT))
                                cnt += 1
                    if final:
                        for tt in range(HB // TT):
                            final_tile(hb * HB + tt * TT, obf)
            P.barrier()

        def phase_CD(do_D=True):
            with contextlib.ExitStack() as st:
                kslc = [sbuf(st, "kslc%d" % g, [128, S], BF16) for g in range(2)]
                kwin = [sbuf(st, "kwin%d" % g, [128, S], BF16) for g in range(2)]
                kcmp = [sbuf(st, "kcmp%d" % g, [128, 128], BF16) for g in range(2)]
                vslc = sbuf(st, "vslc", [128, 2, 16, 65], BF16)
                vwin = sbuf(st, "vwin", [128, 2, 16, 65], BF16)
                vcmp = [sbuf(st, "vcmp%d" % g, [128, 97], BF16) for g in range(2)]
                for g in range(2):
                    P.dve(lambda e, g=g: e.memset(kslc[g][64:128, :], 0.0), writes=[("kslc_aug", g)])
                    P.dma("pool", lambda e, g=g: e.dma_start(out=kslc[g][64:71, :], in_=cst["c_kaug"]), writes=[("kslc_aug", g)])
                    P.dma("pool", lambda e, g=g: e.dma_start(out=kslc[g][96:128, :], in_=cst["c_E"]), writes=[("kslc_aug", g)])
                    P.dma("pool", lambda e, g=g: e.dma_start(out=kwin[g][64:71, :], in_=cst["c_kaug"]), writes=[("kwin_aug", g)])
                    P.dma("pool", lambda e, g=g: e.dma_start(out=kcmp[g][64:71, :], in_=cst["c_kaugc"]), writes=[("kcmp_aug", g)])
                    P.dma("pool", lambda e, g=g: e.dma_start(out=vcmp[g][:, 65:97], in_=cst["c_ovl"]), writes=[("vcmp_c", g)])
                    P.dve(lambda e, g=g: e.memset(vcmp[g][:, 64:65], 1.0), writes=[("vcmp_1", g)])
                P.dve(lambda e: e.memset(vslc[:, :, :, 64:65], 1.0), writes=["vslc_1"])
                P.dve(lambda e: e.memset(vwin[:, :, :, 64:65], 1.0), writes=["vwin_1"])

                pre = phase_D_prefetch(st) if do_D else None
                with contextlib.ExitStack() as st2:
                    wkv = sbuf(st2, "wkv", [128, KC, 768], BF16)
                    xbs = [sbuf(st2, "xbC%d" % i, [128, KC, TT], BF16) for i in range(2)]
                    rawT = sbuf(st2, "rawT", [128, 2, S], BF16)
                    w1kb = sbuf(st2, "w1kb", [128, 32, 64], BF16)
                    w1vb = sbuf(st2, "w1vb", [128, 32, 64], BF16)
                    w2kb = sbuf(st2, "w2kb", [64, 64], BF16)
                    w2vb = sbuf(st2, "w2vb", [64, 64], BF16)
                    pkb = sbuf(st2, "pkb", [64, 32], BF16)
                    pvb = sbuf(st2, "pvb", [64, 32], BF16)
                    cbias = sbuf(st2, "cbias", [64, 2], F32)
                    acmp = sbuf(st2, "acmp", [64, 4, 128], BF16)
                    load_w(wkv, w_kv, KC, "wkv")
                    for hf in range(2):
                        P.dma("pool", lambda e, hf=hf: e.dma_start(out=w1kb[hf * 64:(hf + 1) * 64, :, :], in_=w1k.rearrange("(l d) o -> d l o", d=64)),
                              writes=[("w1kb", hf)])
                        P.dma("pool", lambda e, hf=hf: e.dma_start(out=w1vb[hf * 64:(hf + 1) * 64, :, :], in_=w1v.rearrange("(l d) o -> d l o", d=64)),
                              writes=[("w1vb", hf)])
                    P.dma("pool", lambda e: e.dma_start(out=w2kb[:], in_=w2k), writes=["w2kb"])
                    P.dma("pool", lambda e: e.dma_start(out=w2vb[:], in_=w2v), writes=["w2vb"])
                    P.dma("pool", lambda e: e.dma_start(out=pkb[:], in_=posk), writes=["pkb"])
                    P.dma("pool", lambda e: e.dma_start(out=pvb[:], in_=posv), writes=["pvb"])
                    if pre is not None:
                        pre["loads"]()
                    cnt = 0
                    norm_tile(0, V_KVN, xbs[0], 0, ("xbC", 0))
                    for t in range(NT):
                        t0 = t * TT
                        xb = xbs[t % 2]
                        xk = ("xbC", t % 2)
                        if t + 1 < NT:
                            norm_tile(t0 + TT, V_KVN, xbs[(t + 1) % 2], 0, ("xbC", (t + 1) % 2), part="sq")
                        for part in (0, 1, 2, 4):
                            pp = ps[cnt % 4]
                            kp = ("ps", cnt % 4)
                            cnt += 1
                            for k in range(KC):
                                P.pe(lambda e, k=k, part=part, pp=pp, xb=xb: e.matmul(pp[:, :], lhsT=wkv[:, k, part * 128:(part + 1) * 128], rhs=xb[:, k, :],
                                                                                       start=(k == 0), stop=(k == KC - 1)),
                                     reads=[("wkv", k), (xk, k, 0)], writes=[kp])
                            if part in (0, 1):
                                P.act(lambda e, part=part, pp=pp, t0=t0: e.activation(out=rawT[:, part, t0:t0 + TT], in_=pp[:, :], func=AF.Copy),
                                      reads=[kp], writes=[("rawT", part * 2, t), ("rawT", part * 2 + 1, t)])
                                continue
                            for g in range(2):
                                if part == 2:
                                    dst, dk = kslc[g][0:64, t0:t0 + TT], ("kslc", g, t)
                                else:
                                    dst, dk = kwin[g][0:64, t0:t0 + TT], ("kwin", g, t)
                                P.act(lambda e, g=g, pp=pp, dst=dst: e.activation(out=dst, in_=pp[g * 64:(g + 1) * 64, :], func=AF.Copy),
                                      reads=[kp], writes=[dk])
                        if t + 1 < NT:
                            norm_tile(t0 + TT, V_KVN, xbs[(t + 1) % 2], 0, ("xbC", (t + 1) % 2), part="rest")
                        for j in range(4):
                            kt = t * 4 + j
                            pp = ps[cnt % 4]
                            kp = ("ps", cnt % 4)
                            cnt += 1
                            for ci, c0 in enumerate((384, 640)):
                                for k in range(KC):
                                    P.pe(lambda e, k=k, j=j, ci=ci, c0=c0, pp=pp, xb=xb: e.matmul(pp[:, ci * 128:(ci + 1) * 128], lhsT=xb[:, k, j * 128:(j + 1) * 128],
                                                                                                  rhs=wkv[:, k, c0:c0 + 128], start=(k == 0), stop=(k == KC - 1)),
                                         reads=[("wkv", k), (xk, k, 0)], writes=[kp])
                            P.dve(lambda e, kt=kt, pp=pp: e.tensor_copy(out=vslc[:, :, kt, 0:64], in_=pp[:, 0:128].rearrange("p (g d) -> p g d", g=2)),
                                  reads=[kp], writes=[("vslc", kt)])
                            P.dve(lambda e, kt=kt, pp=pp: e.tensor_copy(out=vwin[:, :, kt, 0:64], in_=pp[:, 128:256].rearrange("p (g d) -> p g d", g=2)),
                                  reads=[kp], writes=[("vwin", kt)])
                    raw_keys = [("rawT", i, t) for i in range(4) for t in range(NT)]
                    for ci, (w1b, pb_) in enumerate(((w1kb, pkb), (w1vb, pvb))):
                        for l in range(32):
                            P.pe(lambda e, l=l, ci=ci, w1b=w1b, pb_=pb_: e.matmul(ps[4][0:64, ci:ci + 1], lhsT=w1b[0:64, l, :], rhs=pb_[:, l:l + 1],
                                                                                  start=(l == 0), stop=(l == 31)),
                                 reads=[("w1kb", 0), ("w1vb", 0), "pkb", "pvb"], writes=[("ps", 4)])
                        P.dve(lambda e, ci=ci: e.tensor_copy(out=cbias[:, ci:ci + 1], in_=ps[4][0:64, ci:ci + 1]),
                              reads=[("ps", 4)], writes=[("cbias", ci)])
                    rv = rawT[:, :, :].rearrange("p i (n s) -> p i n s", s=16)
                    for idx in range(4):
                        ci, g = idx // 2, idx % 2
                        w1b = w1kb if ci == 0 else w1vb
                        pp = ps[idx % 4]
                        kp = ("ps", idx % 4)
                        for l in range(32):
                            P.pe(lambda e, l=l, ci=ci, g=g, w1b=w1b, pp=pp: e.matmul(pp[0:64, 0:NCMP], lhsT=w1b[g * 64:(g + 1) * 64, l, :],
                                                                                     rhs=rv[g * 64:(g + 1) * 64, ci, l // 16:l // 16 + NCMP, l % 16],
                                                                                     start=(l == 0), stop=(l == 31)),
                                 reads=raw_keys + [("w1kb", 0), ("w1kb", 1), ("w1vb", 0), ("w1vb", 1)], writes=[kp])
                        P.act(lambda e, idx=idx, ci=ci, pp=pp: e.activation(out=acmp[:, idx, 0:NCMP], in_=pp[0:64, 0:NCMP], func=AF.Silu,
                                                                            bias=cbias[:, ci:ci + 1]),
                              reads=[kp, ("cbias", ci)], writes=[("acmp", idx)])
                    for g in range(2):
                        pp = ps[4 + g]
                        kp = ("ps", 4 + g)
                        P.pe(lambda e, g=g, pp=pp: e.matmul(pp[0:64, 0:NCMP], lhsT=w2kb[:, :], rhs=acmp[:, g, 0:NCMP], start=True, stop=True),
                             reads=["w2kb", ("acmp", g)], writes=[kp])
                        P.act(lambda e, g=g, pp=pp: e.activation(out=kcmp[g][0:64, 0:NCMP], in_=pp[0:64, 0:NCMP], func=AF.Copy),
                              reads=[kp], writes=[("kcmp", g)])
                        P.pe(lambda e, g=g, pp=pp: e.matmul(pp[0:NCMP, 128:192], lhsT=acmp[:, 2 + g, 0:NCMP], rhs=w2vb[:, :], start=True, stop=True),
                             reads=["w2vb", ("acmp", 2 + g), ("kcmp", g)], writes=[kp])
                        P.dve(lambda e, g=g, pp=pp: e.tensor_copy(out=vcmp[g][0:NCMP, 0:64], in_=pp[0:NCMP, 128:192]),
                              reads=[kp], writes=[("vcmp", g)])
                P.barrier()
                if dbg_kv:
                    with contextlib.ExitStack() as st3:
                        stg = sbuf(st3, "stg", [128, S], F32)
                        stg2 = sbuf(st3, "stg2", [128, 2 * 16 * 65], F32)

                        def dump(src_ap, dst_ap, stage):
                            P.dve(lambda e: e.tensor_copy(out=stage, in_=src_ap), reads=["stg"], writes=["stg"])
                            final_ops.append(P.dma("sp", lambda e: e.dma_start(out=dst_ap, in_=stage), reads=["stg"], writes=["stgd"]))
                            P.barrier()
                        for g in range(2):
                            dump(kslc[g][:, :], dbg["d_kslc"][g], stg[:, :])
                            dump(kwin[g][:, :], dbg["d_kwin"][g], stg[:, :])
                            dump(kcmp[g][:, :], dbg["d_kcmp"][g], stg[:, 0:128])
                            dump(vcmp[g][:, :], dbg["d_vcmp"][g], stg[:, 0:97])
                        dump(vslc[:, :, :, :].rearrange("p g k d -> p (g k d)"), dbg["d_vslc"], stg2[:, :])
                        dump(vwin[:, :, :, :].rearrange("p g k d -> p (g k d)"), dbg["d_vwin"], stg2[:, :])
                if do_D:
                    phase_D(st, kslc, kwin, kcmp, vslc, vwin, vcmp, pre)
            P.barrier()

        def phase_D_prefetch(st):
            pre = {}
            pre["wq"] = wq = sbuf(st, "wq", [128, KC, 1072], BF16)
            pre["wo"] = wo = sbuf(st, "woD", [128, KC, D], BF16)
            pre["qT"] = qT = sbuf(st, "qT", [128, H, TT], BF16)
            pre["triA"] = triA = sbuf(st, "triA", [128, 128], BF16)
            pre["triB"] = triB = sbuf(st, "triB", [128, 128], BF16)
            pre["cmpm"] = cmpm = sbuf(st, "cmpm", [128, 4, TT], BF16)
            pre["cmask"] = cmask = sbuf(st, "cmask", [128, 16, 32], F32)
            pre["amask"] = amask = sbuf(st, "amask", [128, 16, 32], F32)
            pre["abias"] = abias = sbuf(st, "abias", [128, H * 4], F32)
            def loads():
                load_w(wq, w_bin, KC, "wq")
                load_w(wo, w_bout, KC, "woD")
                P.dve(lambda e: e.memset(qT[64:128, :, :], 0.0), writes=["qT_aug"] + [("qsel", g_, j_) for g_ in range(2) for j_ in range(4)])
                P.dma("pool", lambda e: e.dma_start(out=qT[64:71, :, :], in_=cst["c_qaug"]), writes=["qT_aug"])
                P.dma("pool", lambda e: e.dma_start(out=triA[:], in_=cst["c_triA"]), writes=["triA"])
                P.dma("pool", lambda e: e.dma_start(out=triB[:], in_=cst["c_triB"]), writes=["triB"])
                for c in range(4):
                    P.dma("pool", lambda e, c=c: e.dma_start(out=cmpm[:, c, :], in_=cst["c_cmpmask"][:, c, :]), writes=[("cmpm", c)])
                P.dma("sp", lambda e: e.dma_start(out=cmask[:], in_=cst["c_cmask"]), writes=["cmask"])
                P.dma("sp", lambda e: e.dma_start(out=amask[:], in_=cst["c_amask"]), writes=["amask"])
                P.dma("sp", lambda e: e.dma_start(out=abias[:], in_=cst["c_abias"]), writes=["abias"])
            pre["loads"] = loads
            return pre

        def phase_D(st, kslc, kwin, kcmp, vslc, vwin, vcmp, pre):
            wq, wo, qT, triA, triB = pre["wq"], pre["wo"], pre["qT"], pre["triA"], pre["triB"]
            cmpm, cmask, amask, abias = pre["cmpm"], pre["cmask"], pre["amask"], pre["abias"]
            xb = sbuf(st, "xbD", [128, KC, TT], BF16)
            gates = sbuf(st, "gates", [128, 4, 48], F32)
            NPB = 4
            pT = [sbuf(st, "pT%d" % i, [128, TT], BF16) for i in range(NPB)]
            ocomb = sbuf(st, "ocomb", [128, 4, D], F32)
            ocb = sbuf(st, "ocb", [128, 4, D], BF16)
            oT = sq
            grp_bufs = []
            for g_ in range(2):
                grp_bufs.append((sbuf(st, "imp%d" % g_, [128, 4, 32], F32), sbuf(st, "impt%d" % g_, [128, 4, 32], F32),
                                 sbuf(st, "imp2_%d" % g_, [128, 4, 32], F32), sbuf(st, "imp3_%d" % g_, [128, 4, 32], F32),
                                 sbuf(st, "m8a%d" % g_, [128, 4, 8], F32), sbuf(st, "m8b%d" % g_, [128, 4, 8], F32),
                                 sbuf(st, "selb%d" % g_, [128, 4, 32], BF16)))
            evt = [sbuf(st, "evt%d" % i, [128, 4, 64], F32) for i in range(2)]
            den = sbuf(st, "den", [128, 3, 4], F32)
            fsc = sbuf(st, "fsc", [128, 3, 4], F32)
            den_c = sbuf(st, "den_c", [128, 2, 4], F32)
            fsc_c = sbuf(st, "fsc_c", [128, 2, 4], F32)
            imptc = sbuf(st, "imptc", [128, 2, 4, 32], F32)

            sbank = [0, 1, 2]
            OC, OS, OW, MISC = 3, 4, 5, 6
            state = {"s": 0, "p": 0}

            def evac(o3, bi, h, okey, first, last, need_max=False):
                hh = h
                if need_max:
                    P.dve(lambda e: e.tensor_scalar(out=den[:, bi, :], in0=o3[:, :, 64], scalar1=1e-30, scalar2=None, op0=ALU.max),
                          reads=[okey], writes=[("den", bi)])
                    P.dve(lambda e: e.reciprocal(out=den[:, bi, :], in_=den[:, bi, :]), reads=[("den", bi)], writes=[("den", bi)])
                else:
                    P.dve(lambda e: e.reciprocal(out=den[:, bi, :], in_=o3[:, :, 64]), reads=[okey], writes=[("den", bi)])
                P.dve(lambda e: e.tensor_tensor(out=fsc[:, bi, :], in0=den[:, bi, :], in1=gates[:, :, bi * 16 + h], op=ALU.mult),
                      reads=[("den", bi), "gates"], writes=[("fsc", bi)])
                fb = fsc[:, bi, :].unsqueeze(2).to_broadcast([128, 4, 64])
                if first:
                    P.dve(lambda e: e.tensor_tensor(out=ocomb[:, :, hh * 64:(hh + 1) * 64], in0=o3[:, :, 0:64], in1=fb, op=ALU.mult),
                          reads=[okey, ("fsc", bi)], writes=[("ocomb", hh)])
                else:
                    tb = evt[bi % 2]
                    P.dve(lambda e: e.tensor_tensor(out=tb[:, :, :], in0=o3[:, :, 0:64], in1=fb, op=ALU.mult),
                          reads=[okey, ("fsc", bi)], writes=[("evt", bi % 2)])
                    if not last:
                        P.dve(lambda e: e.tensor_tensor(out=ocomb[:, :, hh * 64:(hh + 1) * 64], in0=ocomb[:, :, hh * 64:(hh + 1) * 64],
                                                        in1=tb[:, :, :], op=ALU.add),
                              reads=[("evt", bi % 2), ("ocomb", hh)], writes=[("ocomb", hh)])
                    else:
                        P.dve(lambda e: e.tensor_tensor(out=ocb[:, :, h * 64:(h + 1) * 64], in0=ocomb[:, :, hh * 64:(hh + 1) * 64],
                                                        in1=tb[:, :, :], op=ALU.add),
                              reads=[("evt", bi % 2), ("ocomb", hh)], writes=[("ocb", j_, h) for j_ in range(4)])

            norm_tile(0, V_BN, xb, 0, "xbD", stat_ps=MISC, sq_pool=True)
            for c in range(4):
                t0 = c * TT
                for hp in range(8):
                    qb = hp % 3
                    kp = ("ps", qb)
                    for k in range(KC):
                        P.pe(lambda e, k=k, hp=hp, qb=qb: e.matmul(ps[qb][:, :], lhsT=wq[:, k, hp * 128:(hp + 1) * 128], rhs=xb[:, k, :],
                                                                   start=(k == 0), stop=(k == KC - 1)),
                             reads=[("wq", k), ("xbD", k, 0)], writes=[kp])
                    P.dve(lambda e, hp=hp, qb=qb: e.tensor_copy(out=qT[0:64, 2 * hp, :], in_=ps[qb][0:64, :]), reads=[kp], writes=[("qT", 2 * hp)])
                    P.dve(lambda e, hp=hp, qb=qb: e.tensor_copy(out=qT[0:64, 2 * hp + 1, :], in_=ps[qb][64:128, :]),
                          reads=[kp], writes=[("qT", 2 * hp + 1)])
                for j in range(4):
                    for k in range(KC):
                        P.pe(lambda e, k=k, j=j: e.matmul(ps[MISC][:, j * 48:(j + 1) * 48], lhsT=xb[:, k, j * 128:(j + 1) * 128], rhs=wq[:, k, 1024:1072],
                                                          start=(k == 0), stop=(k == KC - 1)),
                             reads=[("wq", k), ("xbD", k, 0)], writes=[("ps", MISC)])
                P.act(lambda e: e.activation(out=gates[:, :, :], in_=ps[MISC][:, 0:192].rearrange("p (j d) -> p j d", d=48), func=AF.Exp, scale=-1.0),
                      reads=[("ps", MISC)], writes=["gates"])
                P.dve(lambda e: e.tensor_scalar(out=gates[:, :, :], in0=gates[:, :, :], scalar1=1.0, scalar2=None, op0=ALU.add),
                      reads=["gates"], writes=["gates"])
                P.dve(lambda e: e.reciprocal(out=gates[:, :, :], in_=gates[:, :, :]), reads=["gates"], writes=["gates"])

                def do_cmp(g, c=c):
                    imp, impt, imp2, imp3, m8a, m8b, selb = grp_bufs[g]
                    nk = 32 * (c + 1) - 1
                    CB = (OC, MISC, OS, OW)
                    def cmp_score(hh, g=g, c=c, nk=nk):
                        h = g * 8 + hh
                        sb_ = sbank[state["s"] % 3]
                        state["s"] += 1
                        sT = ps[sb_]
                        P.pe(lambda e: e.matmul(sT[0:nk, :], lhsT=kcmp[g][0:71, 0:nk], rhs=qT[0:71, h, :], start=True, stop=False),
                             reads=[("kcmp", g), ("kcmp_aug", g), ("qT", h), "qT_aug"], writes=[("ps", sb_)])
                        P.pe(lambda e: e.matmul(sT[0:nk, :], lhsT=ident[0:nk, 0:nk], rhs=cmpm[0:nk, c, :], start=False, stop=True),
                             reads=["ident", ("cmpm", c)], writes=[("ps", sb_)])
                        return sb_

                    def cmp_rest(hh, sb_, g=g, c=c, nk=nk):
                        h = g * 8 + hh
                        pb_i = state["p"] % NPB
                        state["p"] += 1
                        sT = ps[sb_]
                        eT = pT[pb_i]
                        cbk = CB[hh % 4]
                        ocmp = ps[cbk][:, 0:388].rearrange("p (j d) -> p j d", d=97)
                        P.act(lambda e: e.activation(out=eT[0:nk, :], in_=sT[0:nk, :], func=AF.Exp, scale=0.125,
                                                     bias=abias[0:nk, h * 4 + c:h * 4 + c + 1]),
                              reads=[("ps", sb_), "abias"], writes=[("pT", pb_i)])
                        return pb_i, eT, cbk, ocmp

                    def cmp_pv(hh, pb_i, eT, cbk, ocmp, g=g, nk=nk):
                        h = g * 8 + hh
                        for j in range(4):
                            P.pe(lambda e, j=j: e.matmul(ocmp[:, j, :], lhsT=eT[0:nk, j * 128:(j + 1) * 128], rhs=vcmp[g][0:nk, :],
                                                         start=True, stop=True),
                                 reads=[("pT", pb_i), ("vcmp", g), ("vcmp_c", g), ("vcmp_1", g)], writes=[("ps", cbk)])
                        return (ocmp, h, cbk, hh)

                    def cmp_evac_pair(pair, g=g, c=c):
                        stages = []
                        for s_, (ocmp, h, cbk, hh) in enumerate(pair):
                            st_ = []
                            okey = ("ps", cbk)
                            dk, fk, ik = ("den_c", s_), ("fsc_c", s_), ("imptc", s_)
                            if c == 0:
                                st_.append(lambda ocmp=ocmp, s_=s_, okey=okey, dk=dk: P.dve(
                                    lambda e: e.tensor_scalar(out=den_c[:, s_, :], in0=ocmp[:, :, 64], scalar1=1e-30, scalar2=None, op0=ALU.max),
                                    reads=[okey], writes=[dk]))
                                st_.append(lambda s_=s_, dk=dk: P.dve(
                                    lambda e: e.reciprocal(out=den_c[:, s_, :], in_=den_c[:, s_, :]), reads=[dk], writes=[dk]))
                            else:
                                st_.append(lambda ocmp=ocmp, s_=s_, okey=okey, dk=dk: P.dve(
                                    lambda e: e.reciprocal(out=den_c[:, s_, :], in_=ocmp[:, :, 64]), reads=[okey], writes=[dk]))
                            st_.append(lambda s_=s_, h=h, dk=dk, fk=fk: P.dve(
                                lambda e: e.tensor_tensor(out=fsc_c[:, s_, :], in0=den_c[:, s_, :], in1=gates[:, :, h], op=ALU.mult),
                                reads=[dk, "gates"], writes=[fk]))
                            st_.append(lambda ocmp=ocmp, s_=s_, h=h, okey=okey, fk=fk: P.dve(
                                lambda e: e.tensor_tensor(out=ocomb[:, :, h * 64:(h + 1) * 64], in0=ocmp[:, :, 0:64],
                                                          in1=fsc_c[:, s_, :].unsqueeze(2).to_broadcast([128, 4, 64]), op=ALU.mult),
                                reads=[okey, fk], writes=[("ocomb", h)]))
                            if hh == 0:
                                st_.append(lambda ocmp=ocmp, s_=s_, okey=okey, dk=dk: P.dve(
                                    lambda e: e.tensor_tensor(out=imp[:, :, :], in0=ocmp[:, :, 65:97],
                                                              in1=den_c[:, s_, :].unsqueeze(2).to_broadcast([128, 4, 32]), op=ALU.mult),
                                    reads=[okey, dk], writes=[("imp", g)]))
                            else:
                                st_.append(lambda ocmp=ocmp, s_=s_, okey=okey, dk=dk, ik=ik: P.dve(
                                    lambda e: e.tensor_tensor(out=imptc[:, s_, :, :], in0=ocmp[:, :, 65:97],
                                                              in1=den_c[:, s_, :].unsqueeze(2).to_broadcast([128, 4, 32]), op=ALU.mult),
                                    reads=[okey, dk], writes=[ik]))
                                st_.append(lambda s_=s_, ik=ik: P.dve(
                                    lambda e: e.tensor_tensor(out=imp[:, :, :], in0=imp[:, :, :], in1=imptc[:, s_, :, :], op=ALU.add),
                                    reads=[("imp", g), ik], writes=[("imp", g)]))
                            stages.append(st_)
                        n = max(len(x) for x in stages)
                        for i_ in range(n):
                            for st_ in stages:
                                if i_ < len(st_):
                                    st_[i_]()

                    csb = [None] * 8
                    csb[0] = cmp_score(0)
                    csb[1] = cmp_score(1)
                    pair = []
                    for hh in range(8):
                        r = cmp_rest(hh, csb[hh])
                        if hh + 2 < 8:
                            csb[hh + 2] = cmp_score(hh + 2)
                        pair.append(cmp_pv(hh, *r))
                        if len(pair) == 2:
                            cmp_evac_pair(pair)
                            pair = []
                def topk_dve(g, c=c):
                    imp, impt, imp2, imp3, m8a, m8b, selb = grp_bufs[g]
                    P.dve(lambda e, c=c: e.tensor_tensor(out=imp2[:, :, :], in0=imp[:, :, :], in1=cmask[:, c * 4:(c + 1) * 4, :], op=ALU.mult),
                          reads=[("imp", g), "cmask"], writes=[("imp2", g)])
                    P.dve(lambda e, c=c: e.tensor_tensor(out=imp2[:, :, :], in0=imp2[:, :, :], in1=amask[:, c * 4:(c + 1) * 4, :], op=ALU.add),
                          reads=[("imp2", g), "amask"], writes=[("imp2", g)])
                    for j in range(4):
                        P.dve(lambda e, j=j: e.max(out=m8a[:, j, :], in_=imp2[:, j, :]), reads=[("imp2", g)], writes=[("m8a", g, j)])
                    for j in range(4):
                        P.dve(lambda e, j=j: e.match_replace(out=imp3[:, j, :], in_to_replace=m8a[:, j, :], in_values=imp2[:, j, :], imm_value=-3e9),
                              reads=[("imp2", g), ("m8a", g, j)], writes=[("imp3", g, j)])
                    for j in range(4):
                        P.dve(lambda e, j=j: e.max(out=m8b[:, j, :], in_=imp3[:, j, :]), reads=[("imp3", g, j)], writes=[("m8b", g, j)])
                    P.dve(lambda e: e.tensor_tensor(out=selb[:, :, :], in0=imp2[:, :, :], in1=m8b[:, :, 7:8].to_broadcast([128, 4, 32]), op=ALU.is_lt),
                          reads=[("imp2", g)] + [("m8b", g, j) for j in range(4)], writes=[("selb", g)])
                def topk_pe(g, c=c):
                    imp, impt, imp2, imp3, m8a, m8b, selb = grp_bufs[g]
                    for j in range(4):
                        P.pe(lambda e, j=j: e.transpose(pst[0:32, g * TT + j * 128:g * TT + (j + 1) * 128], selb[:, j, :], ident[:]), reads=[("selb", g), "ident"], writes=[("pstk", g)])
                    P.dve(lambda e, g=g: e.tensor_copy(out=qT[96:128, g * 8:(g + 1) * 8, :],
                                                       in_=pst[0:32, g * TT:(g + 1) * TT].unsqueeze(1).to_broadcast([32, 8, TT])),
                          reads=[("pstk", g)], writes=[("qsel", g, j) for j in range(4)])
                def do_work(g, c=c, inject=None):
                    work = []
                    for hh in range(8):
                        h = g * 8 + hh
                        osb, owb = (OS, OW) if hh % 2 == 0 else (OC, MISC)
                        items = []
                        for kt in range(4 * c + 4):
                            i = kt - 4 * c
                            c0 = 0 if i < 0 else 128 * i
                            items.append(("slc", kt, c0, TT, (c0 if i >= 0 else None), "A"))
                        def win_item(i):
                            kt = 4 * c - 4 + i
                            if i < 4:
                                return ("win", kt, 0, 128 * (i + 1), 128 * i, "B")
                            c0 = 128 * (i - 4)
                            return ("win", kt, c0, TT, c0, "A")
                        if c == 0:
                            for i in range(4, 8):
                                items.append(win_item(i))
                        else:
                            for i in range(3):
                                items.append(("win2", win_item(i), win_item(i + 5)))
                            items.append(win_item(3))
                            items.append(win_item(4))
                        for idx, it in enumerate(items):
                            work.append(dict(it=it, h=h, osb=osb, owb=owb, first=(idx == 0),
                                             last_slc=(it[0] == "slc" and (idx + 1 == len(items) or items[idx + 1][0] != "slc")),
                                             last=(idx == len(items) - 1)))

                    def oview(bank):
                        return ps[bank][:, 0:260].rearrange("p (j d) -> p j d", d=65)

                    def score(w, g=g, c=c):
                        h = w["h"]
                        if w["it"][0] == "win2":
                            sb_ = sbank[state["s"] % 3]
                            state["s"] += 1
                            sT = ps[sb_]
                            for (_b, kt, c0, c1, mc, mk) in w["it"][1:]:
                                tri = triA if mk == "A" else triB
                                P.pe(lambda e, kt=kt, c0=c0, c1=c1: e.matmul(sT[:, c0:c1], lhsT=kwin[g][0:71, kt * 128:(kt + 1) * 128], rhs=qT[0:71, h, c0:c1],
                                                                             start=True, stop=False),
                                     reads=[("kwin", g, kt // 4), ("kwin_aug", g), ("qT", h), "qT_aug"], writes=[("ps", sb_)])
                                P.pe(lambda e, mc=mc, tri=tri: e.matmul(sT[:, mc:mc + 128], lhsT=ident[:, :], rhs=tri[:, :], start=False, stop=True),
                                     reads=["ident", "triA", "triB"], writes=[("ps", sb_)])
                            return sb_
                        br, kt, c0, c1, mc, mk = w["it"]
                        if w["first"]:
                            for bank in (w["osb"], w["owb"]):
                                P.pe(lambda e, bank=bank: e.matmul(ps[bank][:, 0:260], lhsT=zer[0:1, 0:128], rhs=zer[0:1, 0:260], start=True, stop=False),
                                     reads=["zer"], writes=[("ps", bank)])
                        sb_ = sbank[state["s"] % 3]
                        state["s"] += 1
                        sT = ps[sb_]
                        if br == "slc":
                            P.pe(lambda e: e.matmul(sT[:, c0:c1], lhsT=kslc[g][:, kt * 128:(kt + 1) * 128], rhs=qT[:, h, c0:c1],
                                                    start=True, stop=(mc is None)),
                                 reads=[("kslc", g, kt // 4), ("kslc_aug", g), ("qT", h), "qT_aug"] + [("qsel", g, jj) for jj in range(c0 // 128, c1 // 128)],
                                 writes=[("ps", sb_)])
                        else:
                            P.pe(lambda e: e.matmul(sT[:, c0:c1], lhsT=kwin[g][0:71, kt * 128:(kt + 1) * 128], rhs=qT[0:71, h, c0:c1],
                                                    start=True, stop=(mc is None)),
                                 reads=[("kwin", g, kt // 4), ("kwin_aug", g), ("qT", h), "qT_aug"], writes=[("ps", sb_)])
                        if mc is not None:
                            tri = triA if mk == "A" else triB
                            P.pe(lambda e: e.matmul(sT[:, mc:mc + 128], lhsT=ident[:, :], rhs=tri[:, :], start=False, stop=True),
                                 reads=["ident", "triA", "triB"], writes=[("ps", sb_)])
                        return sb_

                    def expo(w, sb_, c=c):
                        if w["it"][0] == "win2":
                            br, kt, c0, c1, mc, mk = ("win", None, 0, TT, None, None)
                        else:
                            br, kt, c0, c1, mc, mk = w["it"]
                        h = w["h"]
                        pb_i = state["p"] % NPB
                        state["p"] += 1
                        P.act(lambda e: e.activation(out=pT[pb_i][:, c0:c1], in_=ps[sb_][:, c0:c1], func=AF.Exp, scale=0.125,
                                                     bias=abias[:, h * 4 + c:h * 4 + c + 1]),
                              reads=[("ps", sb_), "abias"], writes=[("pT", pb_i)])
                        return pb_i

                    def pv(w, pb_i, g=g):
                        if w["it"][0] == "win2":
                            ob = w["owb"]
                            o3 = oview(ob)
                            for (_b, kt, c0, c1, mc, mk) in w["it"][1:]:
                                for j in range(c0 // 128, c1 // 128):
                                    P.pe(lambda e, j=j, kt=kt: e.matmul(o3[:, j, :], lhsT=pT[pb_i][:, j * 128:(j + 1) * 128], rhs=vwin[:, g, kt, :],
                                                                        start=False, stop=False),
                                         reads=[("pT", pb_i), ("vwin", kt), "vwin_1"], writes=[("ps", ob)])
                            return
                        br, kt, c0, c1, mc, mk = w["it"]
                        ob = w["osb"] if br == "slc" else w["owb"]
                        o3 = oview(ob)
                        vv = vslc if br == "slc" else vwin
                        fin = w["last_slc"] if br == "slc" else w["last"]
                        jl = c1 // 128 - 1
                        for j in range(c0 // 128, c1 // 128):
                            P.pe(lambda e, j=j: e.matmul(o3[:, j, :], lhsT=pT[pb_i][:, j * 128:(j + 1) * 128], rhs=vv[:, g, kt, :],
                                                         start=False, stop=(fin and j == jl)),
                                 reads=[("pT", pb_i), (("vslc" if br == "slc" else "vwin"), kt), ("vslc_1" if br == "slc" else "vwin_1")],
                                 writes=[("ps", ob)])

                    nw = len(work)
                    sbs = [None] * nw
                    for i0 in range(min(2, nw)):
                        sbs[i0] = score(work[i0])
                    for ii in range(nw):
                        w = work[ii]
                        pb_i = expo(w, sbs[ii])
                        if ii + 2 < nw:
                            sbs[ii + 2] = score(work[ii + 2])
                        pv(w, pb_i)
                        if inject is not None and ii == 12:
                            inject()
                        if w["last_slc"]:
                            evac(oview(w["osb"]), 1, w["h"], ("ps", w["osb"]), False, False)
                        if w["last"]:
                            evac(oview(w["owb"]), 2, w["h"], ("ps", w["owb"]), False, True)
                do_cmp(0)
                do_cmp(1)
                topk_dve(0)
                topk_pe(0)
                topk_dve(1)
                do_work(0, inject=lambda: topk_pe(1))
                do_work(1)
                xb_later = None
                if c + 1 < 4:
                    xb_later = norm_tile(t0 + TT, V_BN, xb, 0, "xbD", stat_ps=MISC, sq_pool=True, defer_xb=True)
                for j in range(4):
                    for m in range(KC):
                        P.pe(lambda e, j=j, m=m: e.transpose(pst[:, m * 128:(m + 1) * 128], ocb[:, j, m * 128:(m + 1) * 128], ident[:]),
                             reads=[("ocb", j, 2 * m), ("ocb", j, 2 * m + 1), "ident"], writes=[("pstk", 0), ("pstk", 1)])
                    P.dve(lambda e, j=j: e.tensor_copy(out=oT[:, :, j * 128:(j + 1) * 128], in_=pst[:, :].rearrange("p (m q) -> p m q", q=128)),
                          reads=[("pstk", 0), ("pstk", 1)], writes=[("sq", k_) for k_ in range(KC)])
                if xb_later is not None:
                    xb_later()
                for mo in range(KC):
                    ob_ = (MISC, OC, OS, OW)[mo % 4]
                    for m in range(KC):
                        P.pe(lambda e, m=m, mo=mo, ob_=ob_: e.matmul(ps[ob_][:, :], lhsT=wo[:, m, mo * 128:(mo + 1) * 128], rhs=oT[:, m, :],
                                                                     start=(m == 0), stop=(m == KC - 1)),
                             reads=[("woD", m), ("sq", m)], writes=[("ps", ob_)])
                    P.dve(lambda e, mo=mo, t0=t0, ob_=ob_: e.tensor_tensor(out=hT[:, mo, t0:t0 + TT], in0=hT[:, mo, t0:t0 + TT], in1=ps[ob_][:, :], op=ALU.add),
                          reads=[("ps", ob_)] + hkeys(mo, t0, TT), writes=hkeys(mo, t0, TT))

        def phase_out(do_norm):
            with contextlib.ExitStack() as st:
                ob = [sbuf(st, "ob%d" % i, [128, KC, TT], F32) for i in range(2)]
                ov = outT.rearrange("(k p) t -> p k t", p=128)
                for t in range(NT):
                    t0 = t * TT
                    o = ob[t % 2]
                    if do_norm:
                        pst_ = ps[6]
                        for k in range(KC):
                            P.act(lambda e, k=k, t0=t0: e.activation(out=sq[:, k, :], in_=hT[:, k, t0:t0 + TT], func=AF.Square),
                                  reads=hkeys(k, t0, TT), writes=[("sq", k)])
                        for k in range(KC):
                            P.pe(lambda e, k=k: e.matmul(ps[6][:, :], lhsT=ones[:], rhs=sq[:, k, :], start=(k == 0), stop=(k == KC - 1)),
                                 reads=["ones", ("sq", k)], writes=[("ps", 6)])
                        P.act(lambda e: e.activation(out=rstd[:], in_=ps[6][:, :], func=AF.Sqrt, scale=1.0 / D, bias=epsc[:]),
                              reads=[("ps", 6), "epsc"], writes=["rstd"])
                        P.dve(lambda e: e.reciprocal(out=rstd[:], in_=rstd[:]), reads=["rstd"], writes=["rstd"])
                        for k in range(KC):
                            P.dve(lambda e, k=k, o=o, t0=t0: e.scalar_tensor_tensor(out=o[:, k, :], in0=hT[:, k, t0:t0 + TT],
                                                                             scalar=vecs[:, V_FIN + k:V_FIN + k + 1], in1=rstd[:],
                                                                             op0=ALU.mult, op1=ALU.mult),
                                  reads=hkeys(k, t0, TT) + ["rstd", "vecs"], writes=[("ob", t % 2, k)])
                    else:
                        for k in range(KC):
                            P.dve(lambda e, k=k, o=o, t0=t0: e.tensor_copy(out=o[:, k, :], in_=hT[:, k, t0:t0 + TT]),
                                  reads=hkeys(k, t0, TT), writes=[("ob", t % 2, k)])
                    for k in range(KC):
                        final_ops.append(P.dma("sp", lambda e, k=k, o=o, t0=t0: e.dma_start(out=ov[:, k, t0:t0 + TT], in_=o[:, k, :]),
                                               reads=[("ob", t % 2, k)]))

        stages = ["A", "B", "C", "D", "E", "F"]
        last = stages.index(stop_after) if stop_after else len(stages) - 1
        phase_A()
        if last >= 1:
            phase_ffn(0)
        if last >= 2:
            phase_CD(do_D=(last >= 3))
        if last >= 4:
            phase_ffn(1, final=(last >= 5))
        if last < 5:
            phase_out(do_norm=False)
        P.emit(final_wait_ops=final_ops)
    return nc, P.stats


_CACHE = {}


def make_in_maps(inputs, consts=None):
    inp = {k: np.asarray(v) for k, v in inputs.items()}
    consts = consts or make_consts()
    vecs = pack_vecs(inp)
    shared = {
        "vecs": vecs,
        "a_pw1_w": np.ascontiguousarray(inp["a_pw1_w"][0], np.float32),
        "a_pw2_w": np.ascontiguousarray(inp["a_pw2_w"][0], np.float32),
        "w_kv": np.ascontiguousarray(inp["w_kv"], np.float32),
        "posTk": np.ascontiguousarray(inp["cmp_pos_k"].T, np.float32),
        "posTv": np.ascontiguousarray(inp["cmp_pos_v"].T, np.float32),
        "phi_k_w1": np.ascontiguousarray(inp["phi_k_w1"], np.float32),
        "phi_k_w2": np.ascontiguousarray(inp["phi_k_w2"], np.float32),
        "phi_v_w1": np.ascontiguousarray(inp["phi_v_w1"], np.float32),
        "phi_v_w2": np.ascontiguousarray(inp["phi_v_w2"], np.float32),
        "b_w_in": np.ascontiguousarray(inp["b_w_in"][0], np.float32),
        "b_w_out": np.ascontiguousarray(inp["b_w_out"][0], np.float32),
        "ffn_w_in0": np.ascontiguousarray(inp["ffn_w_in"][0], np.float32),
        "ffn_w_in1": np.ascontiguousarray(inp["ffn_w_in"][1], np.float32),
        "ffn_w_out0": np.ascontiguousarray(inp["ffn_w_out"][0], np.float32),
        "ffn_w_out1": np.ascontiguousarray(inp["ffn_w_out"][1], np.float32),
    }
    shared.update(consts)
    maps = []
    for b in range(inp["x"].shape[0]):
        m = dict(shared)
        m["xT"] = np.ascontiguousarray(inp["x"][b].T, np.float32)
        maps.append(m)
    return maps


def kernel(**inputs):
    if "nc" not in _CACHE:
        _CACHE["nc"], _CACHE["stats"] = build()
    nc = _CACHE["nc"]
    maps = make_in_maps(inputs)
    n = len(maps)
    res = run_bass_kernel_spmd(nc, maps, core_ids=list(range(n)))
    out = np.stack([np.asarray(r["outT"], np.float32).T for r in res.results], axis=0)
    return np.ascontiguousarray(out, dtype=np.float32)
```
